# Optimizing a Trainium2 kernel written in Bass

```python
import jax, jax.numpy as jnp
from jax import lax
import numpy as np

D_MODEL = 1024
BATCH = 16
SEQ = 4096
DEPTH = 1

CHUNK = 64
Q_BLOCK = 128
FOX_HEADS = 8
FOX_HEAD_DIM = 64
FOX_WIDTH = FOX_HEADS * FOX_HEAD_DIM
MLSTM_HEADS = 4
MLSTM_HEAD_DIM = 128
MLSTM_WIDTH = MLSTM_HEADS * MLSTM_HEAD_DIM
MIX_WIDTH = FOX_WIDTH + MLSTM_WIDTH
CONV_WIDTH = 4
D_FF = 2816
EPS = 1e-6
IN_SIZES = (FOX_WIDTH, FOX_WIDTH, FOX_WIDTH, FOX_HEADS,
            MLSTM_WIDTH, MLSTM_WIDTH, MLSTM_WIDTH, MLSTM_WIDTH, MLSTM_HEADS, MLSTM_HEADS)
D_IN = 3 * FOX_WIDTH + FOX_HEADS + 4 * MLSTM_WIDTH + 2 * MLSTM_HEADS

kernel_name = "fox_mlstm_macaron_hybrid"


def rms_norm(x, g):
    xf = x.astype(jnp.float32)
    y = xf * lax.rsqrt(jnp.mean(xf * xf, axis=-1, keepdims=True) + EPS)
    return (y * g.astype(jnp.float32)).astype(x.dtype)


def head_rms_norm(a, g):
    H, Dh = a.shape[1], a.shape[3]
    af = a.astype(jnp.float32)
    y = af * lax.rsqrt(jnp.mean(af * af, axis=-1, keepdims=True) + EPS)
    return y * g.astype(jnp.float32).reshape(H, 1, Dh)


def swiglu_ffn(x, w_gu, w_down):
    gate, up = jnp.split(x @ w_gu, 2, axis=-1)
    return (jax.nn.silu(gate) * up) @ w_down


def causal_depthwise_conv(x, w):
    K = w.shape[0]
    S = x.shape[1]
    xp = jnp.pad(x, ((0, 0), (K - 1, 0), (0, 0)))
    y = xp[:, 0:S] * w[0]
    for tap in range(1, K):
        y = y + xp[:, tap:tap + S] * w[tap]
    return y


def split_heads(a, n_heads, head_dim):
    B, S, _ = a.shape
    return a.reshape(B, S, n_heads, head_dim).transpose(0, 2, 1, 3)


def merge_heads(a):
    B, H, S, Dh = a.shape
    return a.transpose(0, 2, 1, 3).reshape(B, S, H * Dh)


def forgetting_attention(q, k, v, log_f):
    B, H, S, Dh = q.shape
    F = jnp.cumsum(log_f, axis=-1)
    scale = Dh ** -0.5
    n_blk = S // Q_BLOCK
    qb = q.reshape(B, H, n_blk, Q_BLOCK, Dh).transpose(2, 0, 1, 3, 4)
    Fb = F.reshape(B, H, n_blk, Q_BLOCK).transpose(2, 0, 1, 3)
    key_pos = jnp.arange(S)

    def block(args):
        blk, q_blk, F_blk = args
        q_pos = blk * Q_BLOCK + jnp.arange(Q_BLOCK)
        s = jnp.einsum('bhqd,bhkd->bhqk', q_blk, k, preferred_element_type=jnp.float32) * scale
        s = s + (F_blk[..., :, None] - F[..., None, :])
        s = jnp.where(key_pos[None, :] <= q_pos[:, None], s, -jnp.inf)
        p = jax.nn.softmax(s, axis=-1)
        return jnp.einsum('bhqk,bhkd->bhqd', p.astype(v.dtype), v)

    out = lax.map(block, (jnp.arange(n_blk), qb, Fb))
    return out.transpose(1, 2, 0, 3, 4).reshape(B, H, S, Dh)


def mlstm_chunkwise(q, k, v, i_pre, log_f):
    B, H, S, D = q.shape
    L = CHUNK
    nc = S // L
    f32 = jnp.float32
    q = q.astype(f32) * (D ** -0.5)
    k = k.astype(f32)
    v = v.astype(f32)

    def to_chunks(a):
        a = a.reshape(B, H, nc, L, *a.shape[3:])
        return jnp.moveaxis(a, 2, 0)

    causal = jnp.tril(jnp.ones((L, L), dtype=bool))

    def step(carry, inp):
        C, n, m = carry
        qt, kt, vt, it, ft = inp
        b = jnp.cumsum(ft, axis=-1)
        logw = b[..., :, None] - b[..., None, :] + it[..., None, :]
        logw = jnp.where(causal, logw, -jnp.inf)
        inter = b + m[..., None]
        m_t = jnp.maximum(inter, jnp.max(logw, axis=-1))
        w_intra = jnp.exp(logw - m_t[..., None])
        w_inter = jnp.exp(inter - m_t)
        qk = jnp.einsum('bhld,bhsd->bhls', qt, kt) * w_intra
        num = (w_inter[..., None] * jnp.einsum('bhld,bhde->bhle', qt, C)
               + jnp.einsum('bhls,bhse->bhle', qk, vt))
        nq = w_inter * jnp.einsum('bhld,bhd->bhl', qt, n) + jnp.sum(qk, axis=-1)
        denom = jnp.maximum(jnp.abs(nq), jnp.exp(-m_t))
        h = num / denom[..., None]
        b_last = b[..., -1]
        g = b_last[..., None] - b + it
        m_new = jnp.maximum(b_last + m, jnp.max(g, axis=-1))
        decay = jnp.exp(b_last + m - m_new)
        wk = jnp.exp(g - m_new[..., None])[..., None] * kt
        C_new = decay[..., None, None] * C + jnp.einsum('bhsd,bhse->bhde', wk, vt)
        n_new = decay[..., None] * n + jnp.sum(wk, axis=-2)
        return (C_new, n_new, m_new), h

    init = (jnp.zeros((B, H, D, D), f32), jnp.zeros((B, H, D), f32), jnp.zeros((B, H), f32))
    _, h = lax.scan(step, init, (to_chunks(q), to_chunks(k), to_chunks(v),
                                 to_chunks(i_pre), to_chunks(log_f)))
    return jnp.moveaxis(h, 0, 2).reshape(B, H, S, D)


def hybrid_mixer(h, w_in, fox_f_bias, mlstm_i_bias, mlstm_f_bias, conv_w,
                 fox_out_norm, mlstm_out_norm, w_out):
    f32 = jnp.float32
    proj = h @ w_in
    split_at = [int(c) for c in np.cumsum(IN_SIZES)[:-1]]
    fq, fk, fv, ff, mq, mk, mv, mo, mi, mf = jnp.split(proj, split_at, axis=-1)
    fox_log_f = jax.nn.log_sigmoid((ff + fox_f_bias).astype(f32)).transpose(0, 2, 1)
    fox_o = forgetting_attention(split_heads(fq, FOX_HEADS, FOX_HEAD_DIM),
                                 split_heads(fk, FOX_HEADS, FOX_HEAD_DIM),
                                 split_heads(fv, FOX_HEADS, FOX_HEAD_DIM), fox_log_f)
    fox_y = merge_heads(head_rms_norm(fox_o, fox_out_norm))
    qk_conv = jax.nn.silu(causal_depthwise_conv(jnp.concatenate([mq, mk], axis=-1), conv_w))
    mq_c, mk_c = jnp.split(qk_conv, 2, axis=-1)
    m_i = (mi + mlstm_i_bias).astype(f32).transpose(0, 2, 1)
    m_logf = jax.nn.log_sigmoid((mf + mlstm_f_bias).astype(f32)).transpose(0, 2, 1)
    ml_h = mlstm_chunkwise(split_heads(mq_c, MLSTM_HEADS, MLSTM_HEAD_DIM),
                           split_heads(mk_c, MLSTM_HEADS, MLSTM_HEAD_DIM),
                           split_heads(mv, MLSTM_HEADS, MLSTM_HEAD_DIM), m_i, m_logf)
    ml_y = jax.nn.sigmoid(mo.astype(f32)) * merge_heads(head_rms_norm(ml_h, mlstm_out_norm))
    y = jnp.concatenate([fox_y, ml_y], axis=-1).astype(h.dtype)
    return y @ w_out


def setup_inputs(seed: int = 0) -> dict:
    key = jax.random.key(seed)
    ks = jax.random.split(key, 20)
    f32 = jnp.float32

    def gain(k, n):
        return 1.0 + 0.02 * jax.random.normal(k, (DEPTH, n), f32)

    x = jax.random.normal(ks[0], (BATCH, SEQ, D_MODEL), f32)
    ffn1_norm = gain(ks[1], D_MODEL)
    ffn1_w_gu = jax.random.normal(ks[2], (DEPTH, D_MODEL, 2 * D_FF), f32) * D_MODEL ** -0.5
    ffn1_w_down = jax.random.normal(ks[3], (DEPTH, D_FF, D_MODEL), f32) * D_FF ** -0.5
    mix_norm = gain(ks[4], D_MODEL)
    w_in = jax.random.normal(ks[5], (DEPTH, D_MODEL, D_IN), f32) * D_MODEL ** -0.5
    fox_f_bias = (jnp.linspace(2.0, 5.0, FOX_HEADS, dtype=f32)[None, :]
                  + 0.1 * jax.random.normal(ks[6], (DEPTH, FOX_HEADS), f32))
    mlstm_i_bias = 0.1 * jax.random.normal(ks[7], (DEPTH, MLSTM_HEADS), f32)
    mlstm_f_bias = (jnp.linspace(3.0, 6.0, MLSTM_HEADS, dtype=f32)[None, :]
                    + 0.1 * jax.random.normal(ks[8], (DEPTH, MLSTM_HEADS), f32))
    conv_w = jax.random.normal(ks[9], (DEPTH, CONV_WIDTH, 2 * MLSTM_WIDTH), f32) * CONV_WIDTH ** -0.5
    fox_out_norm = gain(ks[10], FOX_WIDTH)
    mlstm_out_norm = gain(ks[11], MLSTM_WIDTH)
    w_out = jax.random.normal(ks[12], (DEPTH, MIX_WIDTH, D_MODEL), f32) * MIX_WIDTH ** -0.5
    ffn2_norm = gain(ks[13], D_MODEL)
    ffn2_w_gu = jax.random.normal(ks[14], (DEPTH, D_MODEL, 2 * D_FF), f32) * D_MODEL ** -0.5
    ffn2_w_down = jax.random.normal(ks[15], (DEPTH, D_FF, D_MODEL), f32) * D_FF ** -0.5
    final_norm = 1.0 + 0.02 * jax.random.normal(ks[16], (D_MODEL,), f32)
    return {"x": x, "ffn1_norm": ffn1_norm, "ffn1_w_gu": ffn1_w_gu, "ffn1_w_down": ffn1_w_down,
            "mix_norm": mix_norm, "w_in": w_in, "fox_f_bias": fox_f_bias,
            "mlstm_i_bias": mlstm_i_bias, "mlstm_f_bias": mlstm_f_bias, "conv_w": conv_w,
            "fox_out_norm": fox_out_norm, "mlstm_out_norm": mlstm_out_norm, "w_out": w_out,
            "ffn2_norm": ffn2_norm, "ffn2_w_gu": ffn2_w_gu, "ffn2_w_down": ffn2_w_down,
            "final_norm": final_norm}


def reference(x, ffn1_norm, ffn1_w_gu, ffn1_w_down, mix_norm, w_in, fox_f_bias,
              mlstm_i_bias, mlstm_f_bias, conv_w, fox_out_norm, mlstm_out_norm, w_out,
              ffn2_norm, ffn2_w_gu, ffn2_w_down, final_norm):
    for l in range(DEPTH):
        x = x + 0.5 * swiglu_ffn(rms_norm(x, ffn1_norm[l]), ffn1_w_gu[l], ffn1_w_down[l])
        x = x + hybrid_mixer(rms_norm(x, mix_norm[l]), w_in[l], fox_f_bias[l], mlstm_i_bias[l],
                             mlstm_f_bias[l], conv_w[l], fox_out_norm[l], mlstm_out_norm[l], w_out[l])
        x = x + 0.5 * swiglu_ffn(rms_norm(x, ffn2_norm[l]), ffn2_w_gu[l], ffn2_w_down[l])
    return rms_norm(x, final_norm)
```

```python
import contextlib
import numpy as np
import ml_dtypes
import concourse.bass as bass
import concourse.mybir as mybir
from concourse.bass_utils import run_bass_kernel_spmd

F32 = mybir.dt.float32
BF16 = mybir.dt.bfloat16
AF = mybir.ActivationFunctionType
ALU = mybir.AluOpType
AX = mybir.AxisListType

D = 1024
DFF = 2816
NJ = DFF // 128
EPS = 1e-6
T = 512
NSUB = 4


class Prog:
    def __init__(self):
        self.ops = {e: [] for e in ("pe", "act", "dve", "pool", "sp")}
        self.count = {}
        self.last_w = {}
        self.readers = {}
        self.final_tokens = []

    def op(self, eng, fn, reads=(), writes=(), sem=None, inc=1, per_ins=False):
        waits = {}

        def add(tok):
            if tok is None:
                return
            k, v = tok
            if eng == "pe" and k == "pe":
                return
            if waits.get(k, 0) < v:
                waits[k] = v

        for k in reads:
            add(self.last_w.get(k))
        for k in writes:
            add(self.last_w.get(k))
            for sk, sv in self.readers.get(k, {}).items():
                add((sk, sv))
        semkey = sem if sem is not None else eng
        self.count[semkey] = self.count.get(semkey, 0) + inc
        tok = (semkey, self.count[semkey])
        self.ops[eng].append((fn, waits, semkey, per_ins))
        for k in writes:
            self.last_w[k] = tok
            self.readers[k] = {}
        for k in reads:
            r = self.readers.setdefault(k, {})
            if r.get(tok[0], 0) < tok[1]:
                r[tok[0]] = tok[1]
        return tok

    def emit(self, eng, engine, sems):
        waited = {}
        for fn, waits, semkey, per_ins in self.ops[eng]:
            for k, v in waits.items():
                if waited.get(k, 0) < v:
                    engine.wait_ge(sems[k], v)
                    waited[k] = v
            r = fn(engine)
            if per_ins:
                for ins in r:
                    ins.then_inc(sems[semkey], 16)
            else:
                r.then_inc(sems[semkey], 1)


def build(S, NSEQ, dbg_stop=False):
    NT = S // T
    NKB = S // 128
    nc = bass.Bass("TRN2", target_bir_lowering=False)
    P = Prog()
    es = contextlib.ExitStack()

    def dram(name, shape, dt, kind):
        return nc.dram_tensor(name, shape, dt, kind=kind).ap()

    x_d = dram("x", [NSEQ, S, D], F32, "ExternalInput")
    out_d = dram("out", [NSEQ, S, D], F32, "ExternalOutput")
    wgu_d = [dram("wgu1", [D, 2 * DFF], F32, "ExternalInput"), dram("wgu2", [D, 2 * DFF], F32, "ExternalInput")]
    wd_d = [dram("wd1", [DFF, D], F32, "ExternalInput"), dram("wd2", [DFF, D], F32, "ExternalInput")]
    wfm_d = dram("wfm", [D, 2048], F32, "ExternalInput")
    wtm_d = dram("wtm", [D, 1552], F32, "ExternalInput")
    wo_d = dram("wo", [D, D], F32, "ExternalInput")
    gcols_d = dram("gcols", [128, 24], F32, "ExternalInput")
    convc_d = dram("convc", [128, 32], F32, "ExternalInput")
    gfox_d = dram("gfox", [64, 8], F32, "ExternalInput")
    gfin_d = dram("gfin", [128, D], F32, "ExternalInput")
    gml_d = dram("gml", [128, 512], F32, "ExternalInput")
    biasb_d = dram("biasb", [128, 16], F32, "ExternalInput")
    cf32_d = dram("cf32", [128, 5, 128], F32, "ExternalInput")
    cbf_d = dram("cbf", [128, 4, 128], BF16, "ExternalInput")
    NSLAB = 2 * NJ + 8 + 6 + 6
    wscr = dram("wscr", [NSLAB, 128, 2048], BF16, "Internal")
    wscrd = dram("wscrd", [4 * (NJ // 2), 128, 1024], BF16, "Internal")

    def sb(name, shape, dt):
        return es.enter_context(nc.sbuf_tensor(name, shape, dt))

    xt = sb("xt", [128, NSUB, D], F32)
    ost = sb("ost", [128, 2, D], F32)
    hb = sb("hb", [128, NSUB, D], BF16)
    hT = sb("hT", [128, 8, T], BF16)
    scr = sb("scr", [128, 24, T], BF16)
    sg = sb("sg", [128, 2, T], F32)
    KT = sb("KT", [128, 4, S], BF16)
    Vc = sb("Vc", [128, NKB, 8, 65], BF16)
    pre = sb("pre", [128, 2, T + 3], F32)
    hist = sb("hist", [128, 8, 3], F32)
    cvt = sb("cvt", [128, 1, T], F32)
    mV = sb("mV", [128, NSUB, 4, 129], BF16)
    osig = sb("osig", [128, NSUB, 512], F32)
    gat = sb("gat", [128, NSUB, 16], F32)
    lgf = sb("lgf", [128, NSUB, 16], F32)
    etmp = sb("etmp", [128, NSUB, 16], F32)
    Fcol = sb("Fcol", [128, NKB, 8], F32)
    Cend = sb("Cend", [128, NKB + 1, 8], F32)
    Bt = sb("Bt", [128, NKB, 8], F32)
    D3 = sb("D3", [128, 2, T], BF16)
    dl32 = sb("dl32", [128, 4, 8], F32)
    dlb = sb("dlb", [128, 4, 8], BF16)
    Pp = sb("Pp", [128, 4, T], BF16)
    nrm = sb("nrm", [128, 4, T], F32)
    QTz = hb[:].rearrange("p s (a t) -> p (s a) t", a=2)
    C32 = sb("C32", [128, 4, 129], F32)
    Csb = sb("Csb", [128, 4, 129], BF16)
    WT = sb("WT", [128, 4, 128], BF16)
    Ku = sb("Ku", [128, 4, 128], BF16)
    ytok = sb("ytok", [128, 512], BF16)
    ring = sb("ring", [128, 4, 2048], BF16)
    ringd = sb("ringd", [128, 3, 2, 512], BF16)
    wgt = sb("wgt", [128, 8, 16], BF16)
    gcols = sb("gcols_s", [128, 24], F32)
    convc = sb("convc_s", [128, 32], F32)
    gfox = sb("gfox_s", [64, 8], F32)
    gfin = sb("gfin_s", [128, D], F32)
    gml = sb("gml_s", [128, 512], F32)
    biasb = sb("biasb_s", [128, 16], F32)
    cf32 = sb("cf32_s", [128, 5, 128], F32)
    cbf = sb("cbf_s", [128, 4, 128], BF16)
    ssq = sb("ssq", [128, 8], F32)
    rstd = sb("rstd", [128, 8], F32)
    aS = sb("aS", [128, NSUB, 4], F32)
    uS = sb("uS", [128, NSUB, 4], F32)
    eS = sb("eS", [128, NSUB, 4], F32)
    nb = sb("nb", [128, NSUB, 4], F32)
    zbc = sb("zbc", [128, 32], F32)
    rho = sb("rho", [128, 16], F32)
    rhos = sb("rhos", [128, 16], F32)
    sm = sb("sm", [4, 64], F32)
    dg = sb("dg", [4, 32], F32)
    den = sb("den", [128, 4, 8], F32)

    ident_f = cf32[:, 0, :]
    U_f = cf32[:, 1, :]
    ones_f = cf32[:, 2, :]
    maskS = cf32[:, 3, :]
    statW = cf32[:, 4, :]
    ident_b = cbf[:, 0, :]
    mask01 = cbf[:, 1, :]
    sel0 = cbf[:, 2, :]
    shiftO = cbf[:, 3, :]
    selL = cf32[:, 4, 64:128]

    NB = 6
    pb = [es.enter_context(nc.psum_tensor(f"pb{i}", [128, 512], F32)) for i in range(NB)]
    ptb = [es.enter_context(nc.psum_tensor(f"pt{i}", [128, 1024], BF16)) for i in range(2)]

    actT = scr
    QT = lambda c: scr[:, c, :]
    mqT = lambda h: scr[:, 4 + h, :]
    mkT = lambda h: scr[:, 8 + h, :]
    yTm = lambda h: scr[:, 12 + h, :]
    yTf = lambda h: scr[0:64, 16 + h, :]
    yTp = lambda p_: scr[:, p_, :]

    st = {"bank": 0, "open": set(), "ring": 0, "ringd": 0, "sg": 0, "pp": 0, "pth": 0, "ost": 0}

    def bank():
        for _ in range(NB):
            b = st["bank"]
            st["bank"] = (b + 1) % NB
            if b not in st["open"]:
                st["open"].add(b)
                return b
        raise AssertionError("out of PSUM banks")

    def rel(*bs):
        for b in bs:
            st["open"].discard(b)

    def rot(name, n):
        v = st[name]
        st[name] = (v + 1) % n
        return v

    def mm_group(mms, reads, writes):
        def fn(e, mms=mms):
            r = None
            for (o, l, rh, s0, s1) in mms:
                r = e.matmul(o, l, rh, start=s0, stop=s1)
            return r
        return P.op("pe", fn, reads, writes)

    def tr_group(trs, reads, writes):
        def fn(e, trs=trs):
            r = None
            for (o, i, idn) in trs:
                r = e.transpose(o, i, idn)
            return r
        return P.op("pe", fn, reads, writes)

    def act(out, in_, func, reads, writes, bias=None, scale=None, accum=None):
        kw = {}
        if bias is not None:
            kw["bias"] = bias
        if scale is not None:
            kw["scale"] = scale
        if accum is not None:
            kw["accum_out"] = accum
        return P.op("act", lambda e: e.activation(out, in_, func, **kw), reads, writes)

    class _Rec:
        def __getattr__(self, name):
            def f(*a, **k):
                self.call = (name, a, k)
                return self
            return f

    def dve(f, reads, writes):
        rec = _Rec()
        f(rec)
        name, a, k = rec.call
        return P.op("dve", lambda e: getattr(e, name)(*a, **k), reads, writes)

    def wdma(slot_key, sem, dmas, reads=(), writes=()):
        def fn(e, dmas=dmas):
            return [e.dma_start(out=o, in_=i) for (o, i) in dmas]
        return P.op("pool", fn, reads, tuple(writes) + (slot_key,), sem=sem, inc=16 * len(dmas), per_ins=True)

    first = {"v": True}

    def wload(slot_key, sem, stsem, sb_ap, scr_ap, scr_key, dmas):
        if first["v"]:
            wdma(slot_key, sem, dmas)
            spdma(stsem, [(scr_ap, sb_ap)], reads=(slot_key,), writes=(scr_key,))
        else:
            spdma(sem + "h", [(sb_ap, scr_ap)], reads=(scr_key,), writes=(slot_key,))

    def spdma(sem, dmas, reads=(), writes=()):
        def fn(e, dmas=dmas):
            return [e.dma_start(out=o, in_=i) for (o, i) in dmas]
        return P.op("sp", fn, reads, writes, sem=sem, inc=16 * len(dmas), per_ins=True)

    spdma("s_const", [(gcols[:], gcols_d[:, :]), (convc[:], convc_d[:, :]), (gfox[:], gfox_d[:, :]),
                      (gfin[:], gfin_d[:, :]), (gml[:], gml_d[:, :]), (biasb[:], biasb_d[:, :]),
                      (cf32[:], cf32_d[:, :, :]), (cbf[:], cbf_d[:, :, :])],
          writes=("const",))
    wtm_v = wtm_d.rearrange("(kc p) n -> p kc n", p=128)
    wfm_v = wfm_d.rearrange("(kc p) n -> p kc n", p=128)
    wdma("wgt", "s_wgt", [(wgt[:], wtm_v[:, :, 1536:1552])])
    dve(lambda e: e.memset(Vc[:], 1.0), (), tuple(("Vc", kb) for kb in range(NKB)))
    dve(lambda e: e.memset(mV[:], 1.0), (), ("mV",))
    dve(lambda e: e.memset(D3[:], 0.0), (), (("D3", 0), ("D3", 1)))
    dve(lambda e: e.memset(nrm[:], 0.0), (), ("nrmo", "nrm0", "nrm1", "nrm2"))

    CONST = ("const",)

    def norm_T(goff):
        for sub in range(NSUB):
            act(hb[:, sub, :], xt[:, sub, :], AF.Square, (("xt", sub),), (("hb", sub), ("ssq", sub)), accum=ssq[:, sub:sub + 1])
        for sub in range(NSUB):
            act(rstd[:, 4 + sub:5 + sub], ssq[:, sub:sub + 1], AF.Ln, (("ssq", sub),), (("rstd_t", sub),), bias=EPS, scale=1.0 / D)
            act(rstd[:, sub:sub + 1], rstd[:, 4 + sub:5 + sub], AF.Exp, (("rstd_t", sub),), (("rstd", sub),), scale=-0.5)
            dve(lambda e: e.tensor_scalar(hb[:, sub, :], xt[:, sub, :], rstd[:, sub:sub + 1], None, ALU.mult),
                (("xt", sub), ("rstd", sub)), (("hb", sub),))
            h = rot("pth", 2)
            tr_group([(ptb[h][:, kc * 128:(kc + 1) * 128], hb[:, sub, kc * 128:(kc + 1) * 128], ident_b) for kc in range(8)],
                     (("hb", sub),) + CONST, (("pt", h),))
            dve(lambda e: e.tensor_tensor(hT[:, :, sub * 128:(sub + 1) * 128],
                                          ptb[h][:, :].rearrange("p (k t) -> p k t", k=8),
                                          gcols[:, goff:goff + 8].unsqueeze(2).to_broadcast([128, 8, 128]), ALU.mult),
                (("pt", h),) + CONST, tuple(("hT", kc) for kc in range(8)))

    def ffn(idx):
        wv = wgu_d[idx].rearrange("(kc p) n -> p kc n", p=128)
        hT_keys = tuple(("hT", kc) for kc in range(8))
        for j in range(NJ):
            s = rot("ring", 4)
            slab = ring[:, s, :].rearrange("p (a b) -> p a b", a=8)
            sid = idx * NJ + j
            wload(("ring", s), f"s_ring{s}", f"s_rst{s}", ring[:, s, :], wscr[sid, :, :], ("wscr", sid),
                  [(slab[:, :, 0:128], wv[:, :, j * 128:(j + 1) * 128]),
                   (slab[:, :, 128:256], wv[:, :, DFF + j * 128:DFF + (j + 1) * 128])])
            G = bank()
            mm_group([(pb[G][:, :], slab[:, kc, 0:128], hT[:, kc, :], kc == 0, kc == 7) for kc in range(8)],
                     (("ring", s),) + hT_keys, (("pb", G),))
            Ub = bank()
            mm_group([(pb[Ub][:, :], slab[:, kc, 128:256], hT[:, kc, :], kc == 0, kc == 7) for kc in range(8)],
                     (("ring", s),) + hT_keys, (("pb", Ub),))
            i = rot("sg", 2)
            act(sg[:, i, :], pb[G][:, :], AF.Silu, (("pb", G),), (("sg", i),))
            dve(lambda e, i=i, Ub=Ub, j=j: e.tensor_tensor(actT[:, j, :], sg[:, i, :], pb[Ub][:, :], ALU.mult),
                (("sg", i), ("pb", Ub)), (("scr", j),))
            rel(G, Ub)
        for half in range(2):
            banks = [bank() for _ in range(4)]
            assert len(set(banks)) == 4
            for jj in range(NJ // 2):
                s = rot("ringd", 3)
                sid = (idx * 2 + half) * (NJ // 2) + jj
                wload(("ringd", s), f"s_ringd{s}", f"s_rdst{s}", ringd[:, s, :, :].rearrange("p a b -> p (a b)"),
                      wscrd[sid, :, :], ("wscrd", sid),
                      [(ringd[:, s, :, :], wd_d[idx][jj * 256:(jj + 1) * 256, half * 512:(half + 1) * 512]
                        .rearrange("(a p) n -> p a n", p=128))])
                for a in range(2):
                    j = jj * 2 + a
                    mm_group([(pb[banks[sub]][:, :], actT[:, j, sub * 128:(sub + 1) * 128], ringd[:, s, a, :], j == 0, j == NJ - 1)
                              for sub in range(4)],
                             (("ringd", s), ("scr", j)), tuple(("pb", b) for b in banks))
            for sub in range(4):
                b = banks[sub]
                dve(lambda e, sub=sub, b=b, half=half: e.scalar_tensor_tensor(
                    xt[:, sub, half * 512:(half + 1) * 512], pb[b][:, :], 0.5, xt[:, sub, half * 512:(half + 1) * 512],
                    ALU.mult, ALU.add),
                    (("pb", b), ("xt", sub)), (("xt", sub),))
            rel(*banks)

    for seq in range(NSEQ):
        dve(lambda e: e.memset(C32[:], 0.0), (), ("C32",))
        dve(lambda e: e.memset(hist[:], 0.0), (), ("hist",))
        dve(lambda e: e.memset(Cend[:, 0, :], 0.0), (), ("Cend",))
        dve(lambda e: e.memset(sm[:], 0.0), (), ("sm",))
        for tile in range(NT):
            t0 = tile * T
            for sub in range(NSUB):
                spdma(f"s_x{sub}", [(xt[:, sub, :], x_d[seq, t0 + sub * 128:t0 + (sub + 1) * 128, :])],
                      writes=(("xt", sub),))
            norm_T(0)
            ffn(0)
            norm_T(8)
            hT_keys = tuple(("hT", kc) for kc in range(8))
            def fm_stage(cp):
                s = rot("ring", 4)
                slab = ring[:, s, :].rearrange("p (a b) -> p a b", a=8)
                sid = 2 * NJ + cp
                wload(("ring", s), f"s_ring{s}", f"s_rst{s}", ring[:, s, :], wscr[sid, :, :], ("wscr", sid),
                      [(slab[:, :, :], wfm_v[:, :, cp * 256:(cp + 1) * 256])])
                for cc in range(2):
                    ch = cp * 2 + cc
                    B = bank()
                    mm_group([(pb[B][:, :], slab[:, kc, cc * 128:(cc + 1) * 128], hT[:, kc, :], kc == 0, kc == 7)
                              for kc in range(8)], (("ring", s),) + hT_keys, (("pb", B),))
                    if ch < 4:
                        act(QTz[0:64, 2 * ch, :], pb[B][0:64, :], AF.Copy, (("pb", B),), (("hb", ch),))
                        act(QTz[64:128, 2 * ch + 1, :], pb[B][64:128, :], AF.Copy, (("pb", B),), (("hb", ch),))
                        dve(lambda e: e.memset(QTz[64:128, 2 * ch, :], 0.0), (), (("hb", ch),))
                        dve(lambda e: e.memset(QTz[0:64, 2 * ch + 1, :], 0.0), (), (("hb", ch),))
                    elif ch < 8:
                        c = ch - 4
                        dve(lambda e, c=c, B=B: e.tensor_copy(KT[:, c, t0:t0 + T], pb[B][:, :]),
                            (("pb", B),), (("KT", c, tile),))
                    else:
                        ci = ch - 8
                        pi = ci % 2
                        dve(lambda e, pi=pi, ci=ci: e.tensor_copy(pre[:, pi, 0:3], hist[:, ci, :]),
                            ("hist",), (("pre", pi),))
                        act(pre[:, pi, 3:3 + T], pb[B][:, :], AF.Copy, (("pb", B),), (("pre", pi),))
                        dve(lambda e, pi=pi, ci=ci: e.tensor_copy(hist[:, ci, :], pre[:, pi, T:T + 3]),
                            (("pre", pi),), ("hist",))
                        dve(lambda e, pi=pi, ci=ci: e.tensor_scalar(cvt[:, 0, :], pre[:, pi, 0:T],
                                                                   convc[:, ci * 4:ci * 4 + 1], None, ALU.mult),
                            (("pre", pi),) + CONST, (("cvt", 0),))
                        for tap in range(1, 4):
                            dve(lambda e, pi=pi, ci=ci, tap=tap: e.scalar_tensor_tensor(
                                cvt[:, 0, :], pre[:, pi, tap:tap + T], convc[:, ci * 4 + tap:ci * 4 + tap + 1],
                                cvt[:, 0, :], ALU.mult, ALU.add),
                                (("pre", pi), ("cvt", 0)) + CONST, (("cvt", 0),))
                        dst = mqT(ci) if ci < 4 else mkT(ci - 4)
                        act(dst, cvt[:, 0, :], AF.Silu, (("cvt", 0),), (("scr", 4 + ci),))
                    rel(B)
            pre_side = []
            for cp in range(8):
                if cp < 4:
                    fm_stage(cp)
                else:
                    fm_stage(cp)
            def tm_stage(g):
                slots = []
                for kh in range(2):
                    s = rot("ring", 4)
                    slab = ring[:, s, :].rearrange("p (a b) -> p a b", a=4)
                    sid = 2 * NJ + 8 + g * 2 + kh
                    wload(("ring", s), f"s_ring{s}", f"s_rst{s}", ring[:, s, :], wscr[sid, :, :], ("wscr", sid),
                          [(slab[:, :, :], wtm_v[:, kh * 4:(kh + 1) * 4, g * 512:(g + 1) * 512])])
                    slots.append((s, slab))
                for sub in range(4):
                    B = bank()
                    mm_group([(pb[B][:, :], hT[:, kc, sub * 128:(sub + 1) * 128], slots[kc // 4][1][:, kc % 4, :], kc == 0, kc == 7)
                              for kc in range(8)],
                             (("ring", slots[0][0]), ("ring", slots[1][0])) + hT_keys, (("pb", B),))
                    kb = tile * 4 + sub
                    if g == 0:
                        act(Vc[:, kb, :, 0:64], pb[B][:, :].rearrange("p (h d) -> p h d", h=8), AF.Copy,
                            (("pb", B),), (("Vc", kb),))
                    elif g == 1:
                        dve(lambda e, sub=sub, B=B: e.tensor_copy(mV[:, sub, :, 0:128],
                                                                  pb[B][:, :].rearrange("p (h d) -> p h d", h=4)),
                            (("pb", B),), ("mV",))
                    else:
                        act(osig[:, sub, :], pb[B][:, :], AF.Sigmoid, (("pb", B),), (("osig", sub),))
                        dve(lambda e, sub=sub: e.tensor_tensor(osig[:, sub, :], osig[:, sub, :], gml[:], ALU.mult),
                            (("osig", sub),) + CONST, (("osig", sub),))
                    rel(B)
            tm_stage(0)
            tm_stage(1)
            tm_stage(2)
            for sub in range(4):
                B = bank()
                mm_group([(pb[B][:, 0:16], hT[:, kc, sub * 128:(sub + 1) * 128], wgt[:, kc, :], kc == 0, kc == 7)
                          for kc in range(8)], ("wgt",) + hT_keys, (("pb", B),))
                dve(lambda e, sub=sub, B=B: e.tensor_tensor(gat[:, sub, :], pb[B][:, 0:16], biasb[:], ALU.add),
                    (("pb", B),) + CONST, ("gat",))
                rel(B)
            act(etmp[:], gat[:], AF.Exp, ("gat",), ("etmp",), scale=-1.0)
            act(lgf[:], etmp[:], AF.Ln, ("etmp",), ("lgf",), bias=1.0)
            dve(lambda e: e.tensor_scalar(lgf[:], lgf[:], -1.0, None, ALU.mult), ("lgf",), ("lgf",))

            for sub in range(4):
                kb = tile * 4 + sub
                B = bank()
                mm_group([(pb[B][:, 0:8], U_f, lgf[:, sub, 0:8], True, True),
                          (pb[B][:, 8:16], ones_f, lgf[:, sub, 0:8], True, True)],
                         ("lgf",) + CONST, (("pb", B),))
                dve(lambda e, kb=kb, B=B: e.tensor_tensor(Fcol[:, kb, :], pb[B][:, 0:8], Cend[:, kb, :], ALU.add),
                    (("pb", B), "Cend"), ("Fcol",))
                dve(lambda e, kb=kb, B=B: e.tensor_tensor(Cend[:, kb + 1, :], pb[B][:, 8:16], Cend[:, kb, :], ALU.add),
                    (("pb", B), "Cend"), ("Cend",))
                rel(B)
            qe = tile * 4 + 4
            dve(lambda e: e.scalar_tensor_tensor(Bt[:, 0:qe, :], Fcol[:, 0:qe, :], -1.0,
                                                 Cend[:, qe:qe + 1, :].to_broadcast([128, qe, 8]), ALU.mult, ALU.add),
                ("Fcol", "Cend"), ("Bt",))
            dve(lambda e: e.tensor_tensor(dl32[:], Cend[:, qe - 3:qe + 1, :],
                                          Cend[:, qe:qe + 1, :].to_broadcast([128, 4, 8]), ALU.subtract),
                ("Cend",), ("dl32",))
            dve(lambda e: e.tensor_scalar(dlb[:], dl32[:], 8.0, None, ALU.mult), ("dl32",), ("dlb",))

            B = bank()
            mm_group([(pb[B][:, sub * 4:(sub + 1) * 4], U_f, lgf[:, sub, 12:16], True, True) for sub in range(4)],
                     ("lgf",) + CONST, (("pb", B),))
            bloc = B
            dve(lambda e, B=B: e.tensor_tensor(aS[:], gat[:, :, 8:12],
                                               pb[B][:, 0:16].rearrange("p (s h) -> p s h", s=4), ALU.subtract),
                (("pb", B), "gat"), ("aS",))
            B2 = bank()
            tr_group([(pb[B2][0:4, sub * 128:(sub + 1) * 128], aS[:, sub, :], ident_f) for sub in range(4)],
                     ("aS",) + CONST, (("pb", B2),))
            B3 = bank()
            tr_group([(pb[B3][0:4, sub * 128:(sub + 1) * 128], lgf[:, sub, 12:16], ident_f) for sub in range(4)],
                     ("lgf",) + CONST, (("pb", B3),))
            dve(lambda e, B2=B2: e.tensor_reduce(sm[:, 0:4], pb[B2][0:4, :].rearrange("p (s t) -> p s t", s=4), AX.X, ALU.max),
                (("pb", B2),), ("sm",))
            rel(B2)
            dve(lambda e, B3=B3: e.tensor_reduce(sm[:, 4:8], pb[B3][0:4, :].rearrange("p (s t) -> p s t", s=4), AX.X, ALU.add),
                (("pb", B3),), ("sm",))
            rel(B3)
            dve(lambda e: e.tensor_copy(sm[:, 8:9], sm[:, 24:25]), ("sm",), ("sm",))
            dve(lambda e: e.tensor_copy(sm[:, 9:12], sm[:, 4:7]), ("sm",), ("sm",))
            if tile == 0:
                pass
            dve(lambda e: e.tensor_tensor_scan(sm[:, 12:16], sm[:, 8:12], sm[:, 0:4], sm[:, 25:26], ALU.add, ALU.max),
                ("sm",), ("sm",))
            dve(lambda e: e.tensor_tensor(sm[:, 16:17], sm[:, 8:9], sm[:, 25:26], ALU.add), ("sm",), ("sm",))
            dve(lambda e: e.tensor_tensor(sm[:, 17:20], sm[:, 9:12], sm[:, 12:15], ALU.add), ("sm",), ("sm",))
            dve(lambda e: e.tensor_tensor(sm[:, 20:24], sm[:, 16:20], sm[:, 12:16], ALU.subtract), ("sm",), ("sm",))
            dve(lambda e: e.tensor_copy(sm[:, 24:25], sm[:, 7:8]), ("sm",), ("sm",))
            dve(lambda e: e.tensor_copy(sm[:, 25:26], sm[:, 15:16]), ("sm",), ("sm",))
            for c in range(4):
                dve(lambda e, c=c: e.tensor_scalar(dg[:, c * 4:(c + 1) * 4], ident_f[0:4, 0:4], sm[:, 12 + c:13 + c], None, ALU.mult),
                    ("sm",) + CONST, ("dg",))
                dve(lambda e, c=c: e.tensor_scalar(dg[:, 16 + c * 4:16 + (c + 1) * 4], ident_f[0:4, 0:4], sm[:, 20 + c:21 + c], None, ALU.mult),
                    ("sm",) + CONST, ("dg",))
            B4 = bank()
            mm_group([(pb[B4][:, 0:32], ones_f[0:4, :], dg[:, :], True, True)], ("dg",) + CONST, (("pb", B4),))
            dve(lambda e, B4=B4: e.tensor_copy(zbc[:], pb[B4][:, 0:32]), (("pb", B4),), ("zbc",))
            rel(B4)
            act(rho[:], zbc[:, 16:32], AF.Exp, ("zbc",), ("rho",))
            dve(lambda e: e.tensor_scalar(rhos[:], rho[:], 128.0 ** -0.5, None, ALU.mult), ("rho",), ("rhos",))
            dve(lambda e: e.tensor_tensor(uS[:], aS[:], zbc[:, 0:16].rearrange("p (s h) -> p s h", s=4), ALU.subtract),
                ("aS", "zbc"), ("uS",))
            act(uS[:], uS[:], AF.Exp, ("uS",), ("uS",))
            dve(lambda e, B=bloc: e.scalar_tensor_tensor(nb[:], pb[B][:, 0:16].rearrange("p (s h) -> p s h", s=4), -1.0,
                                                         zbc[:, 0:16].rearrange("p (s h) -> p s h", s=4),
                                                         ALU.mult, ALU.subtract),
                (("pb", bloc), "zbc"), ("nb",))
            rel(bloc)
            act(eS[:], nb[:], AF.Exp, ("nb",), ("eS",))


            from collections import deque
            side = deque(pre_side)

            def mlstm_stages(sub, h):
                cs = slice(sub * 128, (sub + 1) * 128)
                ch = sub * 4 + h
                bk = {}
                CA, CN, CD = slice(0, 128), slice(128, 257), slice(257, 386)

                def s1():
                    X = bank(); bk["X"] = X
                    dve(lambda e: e.tensor_scalar(Csb[:, h, :], C32[:, h, :], rhos[:, ch:ch + 1], None, ALU.mult),
                        (("C32", h), "rhos"), (("Csb", h),))
                    mm_group([(pb[X][:, CA], mkT(h)[:, cs], mqT(h)[:, cs], True, True)],
                             (("scr", 8 + h), ("scr", 4 + h)), (("pb", X),))
                    hh = rot("pth", 2); bk["t"] = hh
                    tr_group([(ptb[hh][:, 0:128], mkT(h)[:, cs], ident_b)], (("scr", 8 + h),) + CONST, (("pt", hh),))

                def s2():
                    X = bk["X"]; hh = bk["t"]
                    dve(lambda e: e.scalar_tensor_tensor(WT[:, h, :], pb[X][:, CA], uS[:, sub, h:h + 1],
                                                         maskS, ALU.mult, ALU.mult),
                        (("pb", X), "uS") + CONST, (("WT", h),))
                    dve(lambda e: e.tensor_scalar(Ku[:, h, :], ptb[hh][:, 0:128], uS[:, sub, h:h + 1], None, ALU.mult),
                        (("pt", hh), "uS"), (("Ku", h),))

                def s3():
                    X = bk["X"]
                    mm_group([(pb[X][:, CN], WT[:, h, :], mV[:, sub, h, :], True, False),
                              (pb[X][:, CN], mqT(h)[:, cs], Csb[:, h, :], False, True),
                              (pb[X][:, CD], Ku[:, h, :], mV[:, sub, h, :], True, True)],
                             (("WT", h), "mV", ("scr", 4 + h), ("Csb", h), ("Ku", h)), (("pb", X),))

                def s4():
                    X = bk["X"]
                    dve(lambda e: e.scalar_tensor_tensor(C32[:, h, :], C32[:, h, :], rho[:, ch:ch + 1],
                                                         pb[X][:, CD], ALU.mult, ALU.add),
                        (("pb", X), ("C32", h), "rho"), (("C32", h),))
                    nqc = pb[X][:, 256:257]
                    dve(lambda e: e.tensor_scalar(den[:, h, 6:7], nqc, -1.0, eS[:, sub, h:h + 1], ALU.mult, ALU.max),
                        (("pb", X), "eS"), (("den6", h),))
                    dve(lambda e: e.tensor_tensor(den[:, h, 0:1], den[:, h, 6:7], nqc, ALU.max),
                        (("pb", X), ("den6", h)), (("den", h),))
                    dve(lambda e: e.reciprocal(den[:, h, 1:2], den[:, h, 0:1]), (("den", h),), (("den", h),))

                def s5():
                    X = bk["X"]
                    act(ytok[:, h * 128:(h + 1) * 128], pb[X][:, 128:256], AF.Square,
                        (("pb", X), ("den", h)), (("ytok", h), ("den2", h)), scale=den[:, h, 1:2], accum=den[:, h, 2:3])
                    act(den[:, h, 3:4], den[:, h, 2:3], AF.Ln, (("den2", h),), (("den3", h),), bias=EPS, scale=1.0 / 128)
                    act(den[:, h, 4:5], den[:, h, 3:4], AF.Exp, (("den3", h),), (("den4", h),), scale=-0.5)

                def s6():
                    X = bk["X"]
                    dve(lambda e: e.tensor_tensor(den[:, h, 5:6], den[:, h, 4:5], den[:, h, 1:2], ALU.mult),
                        (("den4", h), ("den", h)), (("den5", h),))
                    dve(lambda e: e.scalar_tensor_tensor(ytok[:, h * 128:(h + 1) * 128], pb[X][:, 128:256],
                                                         den[:, h, 5:6], osig[:, sub, h * 128:(h + 1) * 128],
                                                         ALU.mult, ALU.mult),
                        (("pb", X), ("den5", h), ("osig", sub)), (("ytok", h),))
                    rel(X)

                def s7():
                    hh = rot("pth", 2); bk["t"] = hh
                    tr_group([(ptb[hh][:, 0:128], ytok[:, h * 128:(h + 1) * 128], ident_b)],
                             (("ytok", h),) + CONST, (("pt", hh),))

                def s8():
                    hh = bk["t"]
                    dve(lambda e: e.tensor_copy(yTm(h)[:, cs], ptb[hh][:, 0:128]), (("pt", hh),), (("scr", 12 + h),))

                return [s1, s2, s3, s4, s5, s6, s7, s8]

            def tail_stages(h):
                def t1():
                    Lb = bank()
                    mm_group([(pb[Lb][0:64, :], selL, nrm[:, 3, :], True, True)],
                             ("nrmo",) + CONST, (("pb", Lb),))
                    dve(lambda e: e.reciprocal(nrm[0:64, 0, :], pb[Lb][0:64, :]), (("pb", Lb),), ("nrm0",))
                    rel(Lb)

                def t2():
                    dve(lambda e: e.tensor_tensor(nrm[0:64, 1, :], nrm[0:64, 3, :], nrm[0:64, 0, :], ALU.mult),
                        ("nrmo", "nrm0"), ("nrm1",))
                    act(nrm[0:64, 2, :], nrm[0:64, 1, :], AF.Square, ("nrm1",), ("nrm2",))

                def t3():
                    Rb = bank()
                    mm_group([(pb[Rb][0:64, :], statW[:, 0:64], nrm[:, 2, :], True, True)], ("nrm2",) + CONST, (("pb", Rb),))
                    act(nrm[0:64, 2, :], pb[Rb][0:64, :], AF.Ln, (("pb", Rb),), ("nrm2",), bias=EPS)
                    rel(Rb)
                    act(nrm[0:64, 2, :], nrm[0:64, 2, :], AF.Exp, ("nrm2",), ("nrm2",), scale=-0.5)

                def t4():
                    dve(lambda e: e.scalar_tensor_tensor(yTf(h), nrm[0:64, 1, :], gfox[:, h:h + 1],
                                                         nrm[0:64, 2, :], ALU.mult, ALU.mult),
                        ("nrm1", "nrm2") + CONST, (("scr", 16 + h),))
                def t5():
                    Bp = bank()
                    mm_group([(pb[Bp][:, :], ident_b[0:64, :], yTf(h - 1), True, False),
                              (pb[Bp][:, :], shiftO[0:64, :], yTf(h), False, True)],
                             (("scr", 16 + h - 1), ("scr", 16 + h)) + CONST, (("pb", Bp),))
                    dve(lambda e: e.tensor_copy(yTp(h // 2), pb[Bp][:, :]), (("pb", Bp),), (("scr", h // 2),))
                    rel(Bp)
                return [t1, t2, t3, t4] + ([t5] if h % 2 == 1 else [])

            for sub in range(4):
                for hp in range(2):
                    sts = [mlstm_stages(sub, 2 * hp), mlstm_stages(sub, 2 * hp + 1)]
                    for si in range(8):
                        for q_ in range(2):
                            sts[q_][si]()

            kbmax = tile * 4 + 3
            steps = [(h, kb) for h in range(8) for kb in range(kbmax + 1)]
            main_left = len(steps)
            hctx = {}

            def qk(h, kb):
                c = h // 2
                dsl = h % 2
                if kb == 0:
                    dve(lambda e: e.tensor_copy(D3[0:1, dsl, :].rearrange("p (i c) -> p i c", i=4),
                                                dlb[0:1, :, h:h + 1].to_broadcast([1, 4, 128])),
                        ("dlb",), (("D3", dsl),))
                j = kb - tile * 4
                c0 = max(j, 0) * 128
                Sb = bank()
                mm_group([(pb[Sb][:, c0:T], KT[:, c, kb * 128:(kb + 1) * 128], QTz[:, h, c0:T], True, False),
                          (pb[Sb][:, c0:T], sel0, D3[:, dsl, c0:T], False, True)],
                         (("KT", c, kb // 4), ("hb", c), ("D3", dsl)) + CONST, (("pb", Sb),))
                return Sb, c0

            def softmax_pv(h, kb, Sb, c0):
                if kb == 0:
                    hctx[h] = bank()
                Ob = hctx[h]
                j = kb - tile * 4
                pi = rot("pp", 4)
                act(Pp[:, pi, c0:T], pb[Sb][:, c0:T], AF.Exp,
                    (("pb", Sb), "Bt"), (("Pp", pi),), bias=Bt[:, kb, h:h + 1], scale=0.125)
                rel(Sb)
                if j >= 0:
                    dve(lambda e: e.tensor_tensor(Pp[:, pi, c0:c0 + 128], Pp[:, pi, c0:c0 + 128], mask01, ALU.mult),
                        (("Pp", pi),) + CONST, (("Pp", pi),))
                mm_group([(pb[Ob][0:65, c0:T], Vc[:, kb, h, :], Pp[:, pi, c0:T], kb == 0, kb == kbmax)],
                         (("Vc", kb), ("Pp", pi)), (("pb", Ob),))

            pend = [qk(*steps[0])]
            if len(steps) > 1:
                pend.append(qk(*steps[1]))
            for i, (h, kb) in enumerate(steps):
                if i + 2 < len(steps):
                    pend.append(qk(*steps[i + 2]))
                Sb, c0 = pend.pop(0)
                softmax_pv(h, kb, Sb, c0)
                main_left -= 1
                nside = -(-len(side) // max(main_left, 1)) if main_left > 0 else len(side)
                for _ in range(min(nside, len(side), 3)):
                    side.popleft()()
                if kb == kbmax:
                    Ob = hctx[h]
                    dve(lambda e: e.tensor_copy(nrm[0:65, 3, :], pb[Ob][0:65, :]), (("pb", Ob),), ("nrmo",))
                    rel(Ob)
                    ts = tail_stages(h)
                    nop = lambda: None
                    if kbmax + 1 >= 12 and h < 7:
                        seq_ = [ts[0], nop, nop, nop, nop, ts[1], nop, nop, ts[2], nop, nop, ts[3]] + ([nop, ts[4]] if len(ts) > 4 else [])
                    else:
                        seq_ = ts
                    for stg in reversed(seq_):
                        side.appendleft(stg)
            while side:
                side.popleft()()

            wof_v = wo_d[0:512, :].rearrange("(h p) n -> p h n", p=128)
            wom_v = wo_d[512:1024, :].rearrange("(h p) n -> p h n", p=128)
            for half in range(2):
                hc = slice(half * 512, (half + 1) * 512)
                sl = []
                for g, wv_ in ((0, wof_v), (2, wom_v)):
                    s = rot("ring", 4)
                    slab = ring[:, s, :].rearrange("p (a b) -> p a b", a=4)
                    sid = 2 * NJ + 14 + half * 3 + g
                    wload(("ring", s), f"s_ring{s}", f"s_rst{s}", ring[:, s, :], wscr[sid, :, :], ("wscr", sid),
                          [(slab[:, :, :], wv_[:, :, hc])])
                    sl.append((s, slab))
                for sub in range(4):
                    cs = slice(sub * 128, (sub + 1) * 128)
                    B = bank()
                    mms = [(pb[B][:, :], yTp(p_)[:, cs], sl[0][1][:, p_, :], p_ == 0, False) for p_ in range(4)]
                    mms += [(pb[B][:, :], yTm(h)[:, cs], sl[1][1][:, h, :], False, h == 3) for h in range(4)]
                    mm_group(mms, tuple(("ring", q[0]) for q in sl) + tuple(("scr", k) for k in (0, 1, 2, 3, 12, 13, 14, 15)),
                             (("pb", B),))
                    dve(lambda e, sub=sub, B=B, hc=hc: e.tensor_tensor(xt[:, sub, hc], xt[:, sub, hc], pb[B][:, :], ALU.add),
                        (("pb", B), ("xt", sub)), (("xt", sub),))
                    rel(B)

            if dbg_stop:
                continue
            norm_T(16)
            ffn(1)

            first["v"] = False
            for sub in range(NSUB):
                act(hb[:, sub, :], xt[:, sub, :], AF.Square, (("xt", sub),), (("hb", sub), ("ssq", sub)), accum=ssq[:, sub:sub + 1])
                act(rstd[:, 4 + sub:5 + sub], ssq[:, sub:sub + 1], AF.Ln, (("ssq", sub),), (("rstd_t", sub),), bias=EPS, scale=1.0 / D)
                act(rstd[:, sub:sub + 1], rstd[:, 4 + sub:5 + sub], AF.Exp, (("rstd_t", sub),), (("rstd", sub),), scale=-0.5)
                oi = rot("ost", 2)
                dve(lambda e: e.scalar_tensor_tensor(ost[:, oi, :], xt[:, sub, :], rstd[:, sub:sub + 1],
                                                     gfin[:], ALU.mult, ALU.mult),
                    (("xt", sub), ("rstd", sub)) + CONST, (("ost", oi),))
                tok = spdma(f"s_out{oi}", [(out_d[seq, t0 + sub * 128:t0 + (sub + 1) * 128, :], ost[:, oi, :])],
                            reads=(("ost", oi),))
                P.final_tokens.append(tok)

    last = {}
    for k, v in P.final_tokens:
        last[k] = max(last.get(k, 0), v)
    P.ops["sp"].append((lambda e: e.nop(), dict(last), "s_fin", False))
    P.count["s_fin"] = 1

    semnames = set(P.count.keys())
    sems = {k: es.enter_context(nc.semaphore(str(k))) for k in sorted(semnames)}
    with es:
        with nc.Block() as block:
            @block.tensor
            def _(e):
                P.emit("pe", e, sems)

            @block.scalar
            def _(e):
                P.emit("act", e, sems)

            @block.vector
            def _(e):
                P.emit("dve", e, sems)

            @block.gpsimd
            def _(e):
                P.emit("pool", e, sems)

            @block.sync
            def _(e):
                P.emit("sp", e, sems)
    return nc


def make_consts():
    cf = np.zeros((128, 5, 128), np.float32)
    cf[:, 0, :] = np.eye(128, dtype=np.float32)
    j = np.arange(128)
    tri = (j[:, None] <= j[None, :]).astype(np.float32)
    cf[:, 1, :] = tri
    cf[:, 2, :] = 1.0
    cf[:, 3, :] = tri * np.float32(128.0 ** -0.5)
    cf[0:64, 4, 0:64] = 1.0 / 64.0
    cf[64, 4, 64:128] = 1.0
    cb = np.zeros((128, 4, 128), ml_dtypes.bfloat16)
    cb[:, 0, :] = np.eye(128).astype(ml_dtypes.bfloat16)
    cb[:, 1, :] = tri.astype(ml_dtypes.bfloat16)
    cb[0, 2, :] = 1.0
    for k_ in range(64):
        cb[k_, 3, 64 + k_] = 1.0
    return cf, cb


def prep_shared(inp):
    f = lambda a: np.ascontiguousarray(np.asarray(a, dtype=np.float32))
    w_in = f(inp["w_in"])[0]
    wfm = np.concatenate([w_in[:, 0:512], w_in[:, 512:1024], w_in[:, 1544:2056], w_in[:, 2056:2568]], axis=1)
    wtm = np.concatenate([w_in[:, 1024:1536], w_in[:, 2568:3080], w_in[:, 3080:3592],
                          w_in[:, 1536:1544], w_in[:, 3592:3596], w_in[:, 3596:3600]], axis=1)
    gcols = np.concatenate([f(inp[k])[0].reshape(8, 128).T for k in ("ffn1_norm", "mix_norm", "ffn2_norm")], axis=1)
    conv = f(inp["conv_w"])[0]
    convc = conv.reshape(4, 8, 128).transpose(2, 1, 0).reshape(128, 32)
    gfox = f(inp["fox_out_norm"])[0].reshape(8, 64).T
    gfin = np.tile(f(inp["final_norm"])[None, :], (128, 1))
    gml = np.tile(f(inp["mlstm_out_norm"])[0][None, :], (128, 1))
    bias16 = np.concatenate([f(inp["fox_f_bias"])[0], f(inp["mlstm_i_bias"])[0], f(inp["mlstm_f_bias"])[0]])
    biasb = np.tile(bias16[None, :], (128, 1))
    cf, cb = make_consts()
    c = np.ascontiguousarray
    return {
        "wgu1": f(inp["ffn1_w_gu"])[0], "wgu2": f(inp["ffn2_w_gu"])[0],
        "wd1": f(inp["ffn1_w_down"])[0], "wd2": f(inp["ffn2_w_down"])[0],
        "wfm": c(wfm), "wtm": c(wtm), "wo": f(inp["w_out"])[0],
        "gcols": c(gcols), "convc": c(convc), "gfox": c(gfox), "gfin": c(gfin), "gml": c(gml),
        "biasb": c(biasb), "cf32": cf, "cbf": cb,
    }


def kernel(**inputs):
    x = np.asarray(inputs["x"], dtype=np.float32)
    Bn, S, _ = x.shape
    n = 8
    nseq = Bn // n
    shared = prep_shared(inputs)
    nc = build(S, nseq)
    in_maps = []
    for cix in range(n):
        m = dict(shared)
        m["x"] = np.ascontiguousarray(x[cix * nseq:(cix + 1) * nseq])
        in_maps.append(m)
    res = run_bass_kernel_spmd(nc, in_maps, core_ids=list(range(n)))
    return np.concatenate([np.asarray(r["out"], dtype=np.float32) for r in res.results], axis=0)
```

```python
import contextlib
import numpy as np
import ml_dtypes
import concourse.bass as bass
import concourse.mybir as mybir
from concourse.bass_utils import run_bass_kernel_spmd

F32 = mybir.dt.float32
BF16 = mybir.dt.bfloat16
AF = mybir.ActivationFunctionType
ALU = mybir.AluOpType
AX = mybir.AxisListType

D = 1024
DFF = 2816
NJ = DFF // 128
EPS = 1e-6
T = 512
NSUB = 4


class Prog:
    def __init__(self):
        self.ops = {e: [] for e in ("pe", "act", "dve", "pool", "sp")}
        self.count = {}
        self.last_w = {}
        self.readers = {}
        self.final_tokens = []

    def op(self, eng, fn, reads=(), writes=(), sem=None, inc=1, per_ins=False):
        waits = {}

        def add(tok):
            if tok is None:
                return
            k, v = tok
            if eng == "pe" and k == "pe":
                return
            if waits.get(k, 0) < v:
                waits[k] = v

        for k in reads:
            add(self.last_w.get(k))
        for k in writes:
            add(self.last_w.get(k))
            for sk, sv in self.readers.get(k, {}).items():
                add((sk, sv))
        semkey = sem if sem is not None else eng
        self.count[semkey] = self.count.get(semkey, 0) + inc
        tok = (semkey, self.count[semkey])
        self.ops[eng].append((fn, waits, semkey, per_ins))
        for k in writes:
            self.last_w[k] = tok
            self.readers[k] = {}
        for k in reads:
            r = self.readers.setdefault(k, {})
            if r.get(tok[0], 0) < tok[1]:
                r[tok[0]] = tok[1]
        return tok

    def emit(self, eng, engine, sems):
        waited = {}
        for fn, waits, semkey, per_ins in self.ops[eng]:
            for k, v in waits.items():
                if waited.get(k, 0) < v:
                    engine.wait_ge(sems[k], v)
                    waited[k] = v
            r = fn(engine)
            if per_ins:
                for ins in r:
                    ins.then_inc(sems[semkey], 16)
            else:
                r.then_inc(sems[semkey], 1)


def build(S, NSEQ, dbg_stop=False):
    NT = S // T
    NKB = S // 128
    nc = bass.Bass("TRN2", target_bir_lowering=False)
    P = Prog()
    es = contextlib.ExitStack()

    def dram(name, shape, dt, kind):
        return nc.dram_tensor(name, shape, dt, kind=kind).ap()

    x_d = dram("x", [NSEQ, S, D], F32, "ExternalInput")
    out_d = dram("out", [NSEQ, S, D], F32, "ExternalOutput")
    wgu_d = [dram("wgu1", [D, 2 * DFF], F32, "ExternalInput"), dram("wgu2", [D, 2 * DFF], F32, "ExternalInput")]
    wd_d = [dram("wd1", [DFF, D], F32, "ExternalInput"), dram("wd2", [DFF, D], F32, "ExternalInput")]
    wfm_d = dram("wfm", [D, 2048], F32, "ExternalInput")
    wtm_d = dram("wtm", [D, 1552], F32, "ExternalInput")
    wo_d = dram("wo", [D, D], F32, "ExternalInput")
    gcols_d = dram("gcols", [128, 24], F32, "ExternalInput")
    convc_d = dram("convc", [128, 32], F32, "ExternalInput")
    gfox_d = dram("gfox", [64, 8], F32, "ExternalInput")
    gfin_d = dram("gfin", [128, D], F32, "ExternalInput")
    gml_d = dram("gml", [128, 512], F32, "ExternalInput")
    biasb_d = dram("biasb", [128, 16], F32, "ExternalInput")
    cf32_d = dram("cf32", [128, 5, 128], F32, "ExternalInput")
    cbf_d = dram("cbf", [128, 4, 128], BF16, "ExternalInput")
    NSLAB = 2 * NJ + 8 + 6 + 6
    wscr = dram("wscr", [NSLAB, 128, 2048], BF16, "Internal")
    wscrd = dram("wscrd", [4 * (NJ // 2), 128, 1024], BF16, "Internal")

    def sb(name, shape, dt):
        return es.enter_context(nc.sbuf_tensor(name, shape, dt))

    xt = sb("xt", [128, NSUB, D], F32)
    ost = sb("ost", [128, 2, D], F32)
    hb = sb("hb", [128, NSUB, D], BF16)
    hT = sb("hT", [128, 8, T], BF16)
    scr = sb("scr", [128, 24, T], BF16)
    sg = sb("sg", [128, 2, T], F32)
    KT = sb("KT", [128, 4, S], BF16)
    Vc = sb("Vc", [128, NKB, 8, 65], BF16)
    pre = sb("pre", [128, 2, T + 3], F32)
    hist = sb("hist", [128, 8, 3], F32)
    cvt = sb("cvt", [128, 1, T], F32)
    mV = sb("mV", [128, NSUB, 4, 129], BF16)
    osig = sb("osig", [128, NSUB, 512], F32)
    gat = sb("gat", [128, NSUB, 16], F32)
    lgf = sb("lgf", [128, NSUB, 16], F32)
    etmp = sb("etmp", [128, NSUB, 16], F32)
    Fcol = sb("Fcol", [128, NKB, 8], F32)
    Cend = sb("Cend", [128, NKB + 1, 8], F32)
    Bt = sb("Bt", [128, NKB, 8], F32)
    D3 = sb("D3", [128, 2, T], BF16)
    dl32 = sb("dl32", [128, 4, 8], F32)
    dlb = sb("dlb", [128, 4, 8], BF16)
    Pp = sb("Pp", [128, 4, T], BF16)
    nrm = sb("nrm", [128, 4, T], F32)
    QTz = hb[:].rearrange("p s (a t) -> p (s a) t", a=2)
    C32 = sb("C32", [128, 4, 129], F32)
    Csb = sb("Csb", [128, 4, 129], BF16)
    WT = sb("WT", [128, 4, 128], BF16)
    Ku = sb("Ku", [128, 4, 128], BF16)
    ytok = sb("ytok", [128, 512], BF16)
    ring = sb("ring", [128, 4, 2048], BF16)
    ringd = sb("ringd", [128, 3, 2, 512], BF16)
    wgt = sb("wgt", [128, 8, 16], BF16)
    gcols = sb("gcols_s", [128, 24], F32)
    convc = sb("convc_s", [128, 32], F32)
    gfox = sb("gfox_s", [64, 8], F32)
    gfin = sb("gfin_s", [128, D], F32)
    gml = sb("gml_s", [128, 512], F32)
    biasb = sb("biasb_s", [128, 16], F32)
    cf32 = sb("cf32_s", [128, 5, 128], F32)
    cbf = sb("cbf_s", [128, 4, 128], BF16)
    ssq = sb("ssq", [128, 8], F32)
    rstd = sb("rstd", [128, 8], F32)
    aS = sb("aS", [128, NSUB, 4], F32)
    uS = sb("uS", [128, NSUB, 4], F32)
    eS = sb("eS", [128, NSUB, 4], F32)
    nb = sb("nb", [128, NSUB, 4], F32)
    zbc = sb("zbc", [128, 32], F32)
    rho = sb("rho", [128, 16], F32)
    rhos = sb("rhos", [128, 16], F32)
    sm = sb("sm", [4, 64], F32)
    dg = sb("dg", [4, 32], F32)
    den = sb("den", [128, 4, 8], F32)

    ident_f = cf32[:, 0, :]
    U_f = cf32[:, 1, :]
    ones_f = cf32[:, 2, :]
    maskS = cf32[:, 3, :]
    statW = cf32[:, 4, :]
    ident_b = cbf[:, 0, :]
    mask01 = cbf[:, 1, :]
    sel0 = cbf[:, 2, :]
    shiftO = cbf[:, 3, :]
    selL = cf32[:, 4, 64:128]

    NB = 6
    pb = [es.enter_context(nc.psum_tensor(f"pb{i}", [128, 512], F32)) for i in range(NB)]
    ptb = [es.enter_context(nc.psum_tensor(f"pt{i}", [128, 1024], BF16)) for i in range(2)]

    actT = scr
    QT = lambda c: scr[:, c, :]
    mqT = lambda h: scr[:, 4 + h, :]
    mkT = lambda h: scr[:, 8 + h, :]
    yTm = lambda h: scr[:, 12 + h, :]
    yTf = lambda h: scr[0:64, 16 + h, :]
    yTp = lambda p_: scr[:, p_, :]

    st = {"bank": 0, "open": set(), "ring": 0, "ringd": 0, "sg": 0, "pp": 0, "pth": 0, "ost": 0}

    def bank():
        for _ in range(NB):
            b = st["bank"]
            st["bank"] = (b + 1) % NB
            if b not in st["open"]:
                st["open"].add(b)
                return b
        raise AssertionError("out of PSUM banks")

    def rel(*bs):
        for b in bs:
            st["open"].discard(b)

    def rot(name, n):
        v = st[name]
        st[name] = (v + 1) % n
        return v

    def mm_group(mms, reads, writes):
        def fn(e, mms=mms):
            r = None
            for (o, l, rh, s0, s1) in mms:
                r = e.matmul(o, l, rh, start=s0, stop=s1)
            return r
        return P.op("pe", fn, reads, writes)

    def tr_group(trs, reads, writes):
        def fn(e, trs=trs):
            r = None
            for (o, i, idn) in trs:
                r = e.transpose(o, i, idn)
            return r
        return P.op("pe", fn, reads, writes)

    def act(out, in_, func, reads, writes, bias=None, scale=None, accum=None):
        kw = {}
        if bias is not None:
            kw["bias"] = bias
        if scale is not None:
            kw["scale"] = scale
        if accum is not None:
            kw["accum_out"] = accum
        return P.op("act", lambda e: e.activation(out, in_, func, **kw), reads, writes)

    class _Rec:
        def __getattr__(self, name):
            def f(*a, **k):
                self.call = (name, a, k)
                return self
            return f

    def dve(f, reads, writes):
        rec = _Rec()
        f(rec)
        name, a, k = rec.call
        return P.op("dve", lambda e: getattr(e, name)(*a, **k), reads, writes)

    def wdma(slot_key, sem, dmas, reads=(), writes=()):
        def fn(e, dmas=dmas):
            return [e.dma_start(out=o, in_=i) for (o, i) in dmas]
        return P.op("pool", fn, reads, tuple(writes) + (slot_key,), sem=sem, inc=16 * len(dmas), per_ins=True)

    first = {"v": True}

    def wload(slot_key, sem, stsem, sb_ap, scr_ap, scr_key, dmas):
        if first["v"]:
            wdma(slot_key, sem, dmas)
            spdma(stsem, [(scr_ap, sb_ap)], reads=(slot_key,), writes=(scr_key,))
        else:
            spdma(sem + "h", [(sb_ap, scr_ap)], reads=(scr_key,), writes=(slot_key,))

    def spdma(sem, dmas, reads=(), writes=()):
        def fn(e, dmas=dmas):
            return [e.dma_start(out=o, in_=i) for (o, i) in dmas]
        return P.op("sp", fn, reads, writes, sem=sem, inc=16 * len(dmas), per_ins=True)

    spdma("s_const", [(gcols[:], gcols_d[:, :]), (convc[:], convc_d[:, :]), (gfox[:], gfox_d[:, :]),
                      (gfin[:], gfin_d[:, :]), (gml[:], gml_d[:, :]), (biasb[:], biasb_d[:, :]),
                      (cf32[:], cf32_d[:, :, :]), (cbf[:], cbf_d[:, :, :])],
          writes=("const",))
    wtm_v = wtm_d.rearrange("(kc p) n -> p kc n", p=128)
    wfm_v = wfm_d.rearrange("(kc p) n -> p kc n", p=128)
    wdma("wgt", "s_wgt", [(wgt[:], wtm_v[:, :, 1536:1552])])
    dve(lambda e: e.memset(Vc[:], 1.0), (), tuple(("Vc", kb) for kb in range(NKB)))
    dve(lambda e: e.memset(mV[:], 1.0), (), ("mV",))
    dve(lambda e: e.memset(D3[:], 0.0), (), (("D3", 0), ("D3", 1)))
    dve(lambda e: e.memset(nrm[:], 0.0), (), ("nrmo", "nrm0", "nrm1", "nrm2"))

    CONST = ("const",)

    def norm_T(goff):
        for sub in range(NSUB):
            act(hb[:, sub, :], xt[:, sub, :], AF.Square, (("xt", sub),), (("hb", sub), ("ssq", sub)), accum=ssq[:, sub:sub + 1])
        for sub in range(NSUB):
            act(rstd[:, 4 + sub:5 + sub], ssq[:, sub:sub + 1], AF.Ln, (("ssq", sub),), (("rstd_t", sub),), bias=EPS, scale=1.0 / D)
            act(rstd[:, sub:sub + 1], rstd[:, 4 + sub:5 + sub], AF.Exp, (("rstd_t", sub),), (("rstd", sub),), scale=-0.5)
            dve(lambda e: e.tensor_scalar(hb[:, sub, :], xt[:, sub, :], rstd[:, sub:sub + 1], None, ALU.mult),
                (("xt", sub), ("rstd", sub)), (("hb", sub),))
            h = rot("pth", 2)
            tr_group([(ptb[h][:, kc * 128:(kc + 1) * 128], hb[:, sub, kc * 128:(kc + 1) * 128], ident_b) for kc in range(8)],
                     (("hb", sub),) + CONST, (("pt", h),))
            dve(lambda e: e.tensor_tensor(hT[:, :, sub * 128:(sub + 1) * 128],
                                          ptb[h][:, :].rearrange("p (k t) -> p k t", k=8),
                                          gcols[:, goff:goff + 8].unsqueeze(2).to_broadcast([128, 8, 128]), ALU.mult),
                (("pt", h),) + CONST, tuple(("hT", kc) for kc in range(8)))

    def ffn(idx):
        wv = wgu_d[idx].rearrange("(kc p) n -> p kc n", p=128)
        hT_keys = tuple(("hT", kc) for kc in range(8))
        for j in range(NJ):
            s = rot("ring", 4)
            slab = ring[:, s, :].rearrange("p (a b) -> p a b", a=8)
            sid = idx * NJ + j
            wload(("ring", s), f"s_ring{s}", f"s_rst{s}", ring[:, s, :], wscr[sid, :, :], ("wscr", sid),
                  [(slab[:, :, 0:128], wv[:, :, j * 128:(j + 1) * 128]),
                   (slab[:, :, 128:256], wv[:, :, DFF + j * 128:DFF + (j + 1) * 128])])
            G = bank()
            mm_group([(pb[G][:, :], slab[:, kc, 0:128], hT[:, kc, :], kc == 0, kc == 7) for kc in range(8)],
                     (("ring", s),) + hT_keys, (("pb", G),))
            Ub = bank()
            mm_group([(pb[Ub][:, :], slab[:, kc, 128:256], hT[:, kc, :], kc == 0, kc == 7) for kc in range(8)],
                     (("ring", s),) + hT_keys, (("pb", Ub),))
            i = rot("sg", 2)
            act(sg[:, i, :], pb[G][:, :], AF.Silu, (("pb", G),), (("sg", i),))
            dve(lambda e, i=i, Ub=Ub, j=j: e.tensor_tensor(actT[:, j, :], sg[:, i, :], pb[Ub][:, :], ALU.mult),
                (("sg", i), ("pb", Ub)), (("scr", j),))
            rel(G, Ub)
        for half in range(2):
            banks = [bank() for _ in range(4)]
            assert len(set(banks)) == 4
            for jj in range(NJ // 2):
                s = rot("ringd", 3)
                sid = (idx * 2 + half) * (NJ // 2) + jj
                wload(("ringd", s), f"s_ringd{s}", f"s_rdst{s}", ringd[:, s, :, :].rearrange("p a b -> p (a b)"),
                      wscrd[sid, :, :], ("wscrd", sid),
                      [(ringd[:, s, :, :], wd_d[idx][jj * 256:(jj + 1) * 256, half * 512:(half + 1) * 512]
                        .rearrange("(a p) n -> p a n", p=128))])
                for a in range(2):
                    j = jj * 2 + a
                    mm_group([(pb[banks[sub]][:, :], actT[:, j, sub * 128:(sub + 1) * 128], ringd[:, s, a, :], j == 0, j == NJ - 1)
                              for sub in range(4)],
                             (("ringd", s), ("scr", j)), tuple(("pb", b) for b in banks))
            for sub in range(4):
                b = banks[sub]
                dve(lambda e, sub=sub, b=b, half=half: e.scalar_tensor_tensor(
                    xt[:, sub, half * 512:(half + 1) * 512], pb[b][:, :], 0.5, xt[:, sub, half * 512:(half + 1) * 512],
                    ALU.mult, ALU.add),
                    (("pb", b), ("xt", sub)), (("xt", sub),))
            rel(*banks)

    for seq in range(NSEQ):
        dve(lambda e: e.memset(C32[:], 0.0), (), ("C32",))
        dve(lambda e: e.memset(hist[:], 0.0), (), ("hist",))
        dve(lambda e: e.memset(Cend[:, 0, :], 0.0), (), ("Cend",))
        dve(lambda e: e.memset(sm[:], 0.0), (), ("sm",))
        for tile in range(NT):
            t0 = tile * T
            for sub in range(NSUB):
                spdma(f"s_x{sub}", [(xt[:, sub, :], x_d[seq, t0 + sub * 128:t0 + (sub + 1) * 128, :])],
                      writes=(("xt", sub),))
            norm_T(0)
            ffn(0)
            norm_T(8)
            hT_keys = tuple(("hT", kc) for kc in range(8))
            def fm_stage(cp):
                s = rot("ring", 4)
                slab = ring[:, s, :].rearrange("p (a b) -> p a b", a=8)
                sid = 2 * NJ + cp
                wload(("ring", s), f"s_ring{s}", f"s_rst{s}", ring[:, s, :], wscr[sid, :, :], ("wscr", sid),
                      [(slab[:, :, :], wfm_v[:, :, cp * 256:(cp + 1) * 256])])
                for cc in range(2):
                    ch = cp * 2 + cc
                    B = bank()
                    mm_group([(pb[B][:, :], slab[:, kc, cc * 128:(cc + 1) * 128], hT[:, kc, :], kc == 0, kc == 7)
                              for kc in range(8)], (("ring", s),) + hT_keys, (("pb", B),))
                    if ch < 4:
                        act(QTz[0:64, 2 * ch, :], pb[B][0:64, :], AF.Copy, (("pb", B),), (("hb", ch),))
                        act(QTz[64:128, 2 * ch + 1, :], pb[B][64:128, :], AF.Copy, (("pb", B),), (("hb", ch),))
                        dve(lambda e: e.memset(QTz[64:128, 2 * ch, :], 0.0), (), (("hb", ch),))
                        dve(lambda e: e.memset(QTz[0:64, 2 * ch + 1, :], 0.0), (), (("hb", ch),))
                    elif ch < 8:
                        c = ch - 4
                        dve(lambda e, c=c, B=B: e.tensor_copy(KT[:, c, t0:t0 + T], pb[B][:, :]),
                            (("pb", B),), (("KT", c, tile),))
                    else:
                        ci = ch - 8
                        pi = ci % 2
                        dve(lambda e, pi=pi, ci=ci: e.tensor_copy(pre[:, pi, 0:3], hist[:, ci, :]),
                            ("hist",), (("pre", pi),))
                        act(pre[:, pi, 3:3 + T], pb[B][:, :], AF.Copy, (("pb", B),), (("pre", pi),))
                        dve(lambda e, pi=pi, ci=ci: e.tensor_copy(hist[:, ci, :], pre[:, pi, T:T + 3]),
                            (("pre", pi),), ("hist",))
                        dve(lambda e, pi=pi, ci=ci: e.tensor_scalar(cvt[:, 0, :], pre[:, pi, 0:T],
                                                                   convc[:, ci * 4:ci * 4 + 1], None, ALU.mult),
                            (("pre", pi),) + CONST, (("cvt", 0),))
                        for tap in range(1, 4):
                            dve(lambda e, pi=pi, ci=ci, tap=tap: e.scalar_tensor_tensor(
                                cvt[:, 0, :], pre[:, pi, tap:tap + T], convc[:, ci * 4 + tap:ci * 4 + tap + 1],
                                cvt[:, 0, :], ALU.mult, ALU.add),
                                (("pre", pi), ("cvt", 0)) + CONST, (("cvt", 0),))
                        dst = mqT(ci) if ci < 4 else mkT(ci - 4)
                        act(dst, cvt[:, 0, :], AF.Silu, (("cvt", 0),), (("scr", 4 + ci),))
                    rel(B)
            pre_side = []
            for cp in range(8):
                if cp < 4:
                    fm_stage(cp)
                else:
                    fm_stage(cp)
            def tm_stage(g):
                slots = []
                for kh in range(2):
                    s = rot("ring", 4)
                    slab = ring[:, s, :].rearrange("p (a b) -> p a b", a=4)
                    sid = 2 * NJ + 8 + g * 2 + kh
                    wload(("ring", s), f"s_ring{s}", f"s_rst{s}", ring[:, s, :], wscr[sid, :, :], ("wscr", sid),
                          [(slab[:, :, :], wtm_v[:, kh * 4:(kh + 1) * 4, g * 512:(g + 1) * 512])])
                    slots.append((s, slab))
                for sub in range(4):
                    B = bank()
                    mm_group([(pb[B][:, :], hT[:, kc, sub * 128:(sub + 1) * 128], slots[kc // 4][1][:, kc % 4, :], kc == 0, kc == 7)
                              for kc in range(8)],
                             (("ring", slots[0][0]), ("ring", slots[1][0])) + hT_keys, (("pb", B),))
                    kb = tile * 4 + sub
                    if g == 0:
                        act(Vc[:, kb, :, 0:64], pb[B][:, :].rearrange("p (h d) -> p h d", h=8), AF.Copy,
                            (("pb", B),), (("Vc", kb),))
                    elif g == 1:
                        dve(lambda e, sub=sub, B=B: e.tensor_copy(mV[:, sub, :, 0:128],
                                                                  pb[B][:, :].rearrange("p (h d) -> p h d", h=4)),
                            (("pb", B),), ("mV",))
                    else:
                        act(osig[:, sub, :], pb[B][:, :], AF.Sigmoid, (("pb", B),), (("osig", sub),))
                        dve(lambda e, sub=sub: e.tensor_tensor(osig[:, sub, :], osig[:, sub, :], gml[:], ALU.mult),
                            (("osig", sub),) + CONST, (("osig", sub),))
                    rel(B)
            tm_stage(0)
            tm_stage(1)
            tm_stage(2)
            for sub in range(4):
                B = bank()
                mm_group([(pb[B][:, 0:16], hT[:, kc, sub * 128:(sub + 1) * 128], wgt[:, kc, :], kc == 0, kc == 7)
                          for kc in range(8)], ("wgt",) + hT_keys, (("pb", B),))
                dve(lambda e, sub=sub, B=B: e.tensor_tensor(gat[:, sub, :], pb[B][:, 0:16], biasb[:], ALU.add),
                    (("pb", B),) + CONST, ("gat",))
                rel(B)
            act(etmp[:], gat[:], AF.Exp, ("gat",), ("etmp",), scale=-1.0)
            act(lgf[:], etmp[:], AF.Ln, ("etmp",), ("lgf",), bias=1.0)
            dve(lambda e: e.tensor_scalar(lgf[:], lgf[:], -1.0, None, ALU.mult), ("lgf",), ("lgf",))

            for sub in range(4):
                kb = tile * 4 + sub
                B = bank()
                mm_group([(pb[B][:, 0:8], U_f, lgf[:, sub, 0:8], True, True),
                          (pb[B][:, 8:16], ones_f, lgf[:, sub, 0:8], True, True)],
                         ("lgf",) + CONST, (("pb", B),))
                dve(lambda e, kb=kb, B=B: e.tensor_tensor(Fcol[:, kb, :], pb[B][:, 0:8], Cend[:, kb, :], ALU.add),
                    (("pb", B), "Cend"), ("Fcol",))
                dve(lambda e, kb=kb, B=B: e.tensor_tensor(Cend[:, kb + 1, :], pb[B][:, 8:16], Cend[:, kb, :], ALU.add),
                    (("pb", B), "Cend"), ("Cend",))
                rel(B)
            qe = tile * 4 + 4
            dve(lambda e: e.scalar_tensor_tensor(Bt[:, 0:qe, :], Fcol[:, 0:qe, :], -1.0,
                                                 Cend[:, qe:qe + 1, :].to_broadcast([128, qe, 8]), ALU.mult, ALU.add),
                ("Fcol", "Cend"), ("Bt",))
            dve(lambda e: e.tensor_tensor(dl32[:], Cend[:, qe - 3:qe + 1, :],
                                          Cend[:, qe:qe + 1, :].to_broadcast([128, 4, 8]), ALU.subtract),
                ("Cend",), ("dl32",))
            dve(lambda e: e.tensor_scalar(dlb[:], dl32[:], 8.0, None, ALU.mult), ("dl32",), ("dlb",))

            B = bank()
            mm_group([(pb[B][:, sub * 4:(sub + 1) * 4], U_f, lgf[:, sub, 12:16], True, True) for sub in range(4)],
                     ("lgf",) + CONST, (("pb", B),))
            bloc = B
            dve(lambda e, B=B: e.tensor_tensor(aS[:], gat[:, :, 8:12],
                                               pb[B][:, 0:16].rearrange("p (s h) -> p s h", s=4), ALU.subtract),
                (("pb", B), "gat"), ("aS",))
            B2 = bank()
            tr_group([(pb[B2][0:4, sub * 128:(sub + 1) * 128], aS[:, sub, :], ident_f) for sub in range(4)],
                     ("aS",) + CONST, (("pb", B2),))
            B3 = bank()
            tr_group([(pb[B3][0:4, sub * 128:(sub + 1) * 128], lgf[:, sub, 12:16], ident_f) for sub in range(4)],
                     ("lgf",) + CONST, (("pb", B3),))
            dve(lambda e, B2=B2: e.tensor_reduce(sm[:, 0:4], pb[B2][0:4, :].rearrange("p (s t) -> p s t", s=4), AX.X, ALU.max),
                (("pb", B2),), ("sm",))
            rel(B2)
            dve(lambda e, B3=B3: e.tensor_reduce(sm[:, 4:8], pb[B3][0:4, :].rearrange("p (s t) -> p s t", s=4), AX.X, ALU.add),
                (("pb", B3),), ("sm",))
            rel(B3)
            dve(lambda e: e.tensor_copy(sm[:, 8:9], sm[:, 24:25]), ("sm",), ("sm",))
            dve(lambda e: e.tensor_copy(sm[:, 9:12], sm[:, 4:7]), ("sm",), ("sm",))
            if tile == 0:
                pass
            dve(lambda e: e.tensor_tensor_scan(sm[:, 12:16], sm[:, 8:12], sm[:, 0:4], sm[:, 25:26], ALU.add, ALU.max),
                ("sm",), ("sm",))
            dve(lambda e: e.tensor_tensor(sm[:, 16:17], sm[:, 8:9], sm[:, 25:26], ALU.add), ("sm",), ("sm",))
            dve(lambda e: e.tensor_tensor(sm[:, 17:20], sm[:, 9:12], sm[:, 12:15], ALU.add), ("sm",), ("sm",))
            dve(lambda e: e.tensor_tensor(sm[:, 20:24], sm[:, 16:20], sm[:, 12:16], ALU.subtract), ("sm",), ("sm",))
            dve(lambda e: e.tensor_copy(sm[:, 24:25], sm[:, 7:8]), ("sm",), ("sm",))
            dve(lambda e: e.tensor_copy(sm[:, 25:26], sm[:, 15:16]), ("sm",), ("sm",))
            for c in range(4):
                dve(lambda e, c=c: e.tensor_scalar(dg[:, c * 4:(c + 1) * 4], ident_f[0:4, 0:4], sm[:, 12 + c:13 + c], None, ALU.mult),
                    ("sm",) + CONST, ("dg",))
                dve(lambda e, c=c: e.tensor_scalar(dg[:, 16 + c * 4:16 + (c + 1) * 4], ident_f[0:4, 0:4], sm[:, 20 + c:21 + c], None, ALU.mult),
                    ("sm",) + CONST, ("dg",))
            B4 = bank()
            mm_group([(pb[B4][:, 0:32], ones_f[0:4, :], dg[:, :], True, True)], ("dg",) + CONST, (("pb", B4),))
            dve(lambda e, B4=B4: e.tensor_copy(zbc[:], pb[B4][:, 0:32]), (("pb", B4),), ("zbc",))
            rel(B4)
            act(rho[:], zbc[:, 16:32], AF.Exp, ("zbc",), ("rho",))
            dve(lambda e: e.tensor_scalar(rhos[:], rho[:], 128.0 ** -0.5, None, ALU.mult), ("rho",), ("rhos",))
            dve(lambda e: e.tensor_tensor(uS[:], aS[:], zbc[:, 0:16].rearrange("p (s h) -> p s h", s=4), ALU.subtract),
                ("aS", "zbc"), ("uS",))
            act(uS[:], uS[:], AF.Exp, ("uS",), ("uS",))
            dve(lambda e, B=bloc: e.scalar_tensor_tensor(nb[:], pb[B][:, 0:16].rearrange("p (s h) -> p s h", s=4), -1.0,
                                                         zbc[:, 0:16].rearrange("p (s h) -> p s h", s=4),
                                                         ALU.mult, ALU.subtract),
                (("pb", bloc), "zbc"), ("nb",))
            rel(bloc)
            act(eS[:], nb[:], AF.Exp, ("nb",), ("eS",))


            from collections import deque
            side = deque(pre_side)

            def mlstm_stages(sub, h):
                cs = slice(sub * 128, (sub + 1) * 128)
                ch = sub * 4 + h
                bk = {}
                CA, CN, CD = slice(0, 128), slice(128, 257), slice(257, 386)

                def s1():
                    X = bank(); bk["X"] = X
                    dve(lambda e: e.tensor_scalar(Csb[:, h, :], C32[:, h, :], rhos[:, ch:ch + 1], None, ALU.mult),
                        (("C32", h), "rhos"), (("Csb", h),))
                    mm_group([(pb[X][:, CA], mkT(h)[:, cs], mqT(h)[:, cs], True, True)],
                             (("scr", 8 + h), ("scr", 4 + h)), (("pb", X),))
                    hh = rot("pth", 2); bk["t"] = hh
                    tr_group([(ptb[hh][:, 0:128], mkT(h)[:, cs], ident_b)], (("scr", 8 + h),) + CONST, (("pt", hh),))

                def s2():
                    X = bk["X"]; hh = bk["t"]
                    dve(lambda e: e.scalar_tensor_tensor(WT[:, h, :], pb[X][:, CA], uS[:, sub, h:h + 1],
                                                         maskS, ALU.mult, ALU.mult),
                        (("pb", X), "uS") + CONST, (("WT", h),))
                    dve(lambda e: e.tensor_scalar(Ku[:, h, :], ptb[hh][:, 0:128], uS[:, sub, h:h + 1], None, ALU.mult),
                        (("pt", hh), "uS"), (("Ku", h),))

                def s3():
                    X = bk["X"]
                    mm_group([(pb[X][:, CN], WT[:, h, :], mV[:, sub, h, :], True, False),
                              (pb[X][:, CN], mqT(h)[:, cs], Csb[:, h, :], False, True),
                              (pb[X][:, CD], Ku[:, h, :], mV[:, sub, h, :], True, True)],
                             (("WT", h), "mV", ("scr", 4 + h), ("Csb", h), ("Ku", h)), (("pb", X),))

                def s4():
                    X = bk["X"]
                    dve(lambda e: e.scalar_tensor_tensor(C32[:, h, :], C32[:, h, :], rho[:, ch:ch + 1],
                                                         pb[X][:, CD], ALU.mult, ALU.add),
                        (("pb", X), ("C32", h), "rho"), (("C32", h),))
                    nqc = pb[X][:, 256:257]
                    dve(lambda e: e.tensor_scalar(den[:, h, 6:7], nqc, -1.0, eS[:, sub, h:h + 1], ALU.mult, ALU.max),
                        (("pb", X), "eS"), (("den6", h),))
                    dve(lambda e: e.tensor_tensor(den[:, h, 0:1], den[:, h, 6:7], nqc, ALU.max),
                        (("pb", X), ("den6", h)), (("den", h),))
                    dve(lambda e: e.reciprocal(den[:, h, 1:2], den[:, h, 0:1]), (("den", h),), (("den", h),))

                def s5():
                    X = bk["X"]
                    act(ytok[:, h * 128:(h + 1) * 128], pb[X][:, 128:256], AF.Square,
                        (("pb", X), ("den", h)), (("ytok", h), ("den2", h)), scale=den[:, h, 1:2], accum=den[:, h, 2:3])
                    act(den[:, h, 3:4], den[:, h, 2:3], AF.Ln, (("den2", h),), (("den3", h),), bias=EPS, scale=1.0 / 128)
                    act(den[:, h, 4:5], den[:, h, 3:4], AF.Exp, (("den3", h),), (("den4", h),), scale=-0.5)

                def s6():
                    X = bk["X"]
                    dve(lambda e: e.tensor_tensor(den[:, h, 5:6], den[:, h, 4:5], den[:, h, 1:2], ALU.mult),
                        (("den4", h), ("den", h)), (("den5", h),))
                    dve(lambda e: e.scalar_tensor_tensor(ytok[:, h * 128:(h + 1) * 128], pb[X][:, 128:256],
                                                         den[:, h, 5:6], osig[:, sub, h * 128:(h + 1) * 128],
                                                         ALU.mult, ALU.mult),
                        (("pb", X), ("den5", h), ("osig", sub)), (("ytok", h),))
                    rel(X)

                def s7():
                    hh = rot("pth", 2); bk["t"] = hh
                    tr_group([(ptb[hh][:, 0:128], ytok[:, h * 128:(h + 1) * 128], ident_b)],
                             (("ytok", h),) + CONST, (("pt", hh),))

                def s8():
                    hh = bk["t"]
                    dve(lambda e: e.tensor_copy(yTm(h)[:, cs], ptb[hh][:, 0:128]), (("pt", hh),), (("scr", 12 + h),))

                return [s1, s2, s3, s4, s5, s6, s7, s8]

            def tail_stages(h):
                def t1():
                    Lb = bank()
                    mm_group([(pb[Lb][0:64, :], selL, nrm[:, 3, :], True, True)],
                             ("nrmo",) + CONST, (("pb", Lb),))
                    dve(lambda e: e.reciprocal(nrm[0:64, 0, :], pb[Lb][0:64, :]), (("pb", Lb),), ("nrm0",))
                    rel(Lb)

                def t2():
                    dve(lambda e: e.tensor_tensor(nrm[0:64, 1, :], nrm[0:64, 3, :], nrm[0:64, 0, :], ALU.mult),
                        ("nrmo", "nrm0"), ("nrm1",))
                    act(nrm[0:64, 2, :], nrm[0:64, 1, :], AF.Square, ("nrm1",), ("nrm2",))

                def t3():
                    Rb = bank()
                    mm_group([(pb[Rb][0:64, :], statW[:, 0:64], nrm[:, 2, :], True, True)], ("nrm2",) + CONST, (("pb", Rb),))
                    act(nrm[0:64, 2, :], pb[Rb][0:64, :], AF.Ln, (("pb", Rb),), ("nrm2",), bias=EPS)
                    rel(Rb)
                    act(nrm[0:64, 2, :], nrm[0:64, 2, :], AF.Exp, ("nrm2",), ("nrm2",), scale=-0.5)

                def t4():
                    dve(lambda e: e.scalar_tensor_tensor(yTf(h), nrm[0:64, 1, :], gfox[:, h:h + 1],
                                                         nrm[0:64, 2, :], ALU.mult, ALU.mult),
                        ("nrm1", "nrm2") + CONST, (("scr", 16 + h),))
                def t5():
                    Bp = bank()
                    mm_group([(pb[Bp][:, :], ident_b[0:64, :], yTf(h - 1), True, False),
                              (pb[Bp][:, :], shiftO[0:64, :], yTf(h), False, True)],
                             (("scr", 16 + h - 1), ("scr", 16 + h)) + CONST, (("pb", Bp),))
                    dve(lambda e: e.tensor_copy(yTp(h // 2), pb[Bp][:, :]), (("pb", Bp),), (("scr", h // 2),))
                    rel(Bp)
                return [t1, t2, t3, t4] + ([t5] if h % 2 == 1 else [])

            for sub in range(4):
                for h_ in range(4):
                    for x_ in mlstm_stages(sub, h_):
                        side.append(x_)

            kbmax = tile * 4 + 3
            steps = [(h, kb) for h in range(8) for kb in range(kbmax + 1)]
            main_left = len(steps)
            hctx = {}

            def qk(h, kb):
                c = h // 2
                dsl = h % 2
                if kb == 0:
                    dve(lambda e: e.tensor_copy(D3[0:1, dsl, :].rearrange("p (i c) -> p i c", i=4),
                                                dlb[0:1, :, h:h + 1].to_broadcast([1, 4, 128])),
                        ("dlb",), (("D3", dsl),))
                j = kb - tile * 4
                c0 = max(j, 0) * 128
                Sb = bank()
                mm_group([(pb[Sb][:, c0:T], KT[:, c, kb * 128:(kb + 1) * 128], QTz[:, h, c0:T], True, False),
                          (pb[Sb][:, c0:T], sel0, D3[:, dsl, c0:T], False, True)],
                         (("KT", c, kb // 4), ("hb", c), ("D3", dsl)) + CONST, (("pb", Sb),))
                return Sb, c0

            def softmax_pv(h, kb, Sb, c0):
                if kb == 0:
                    hctx[h] = bank()
                Ob = hctx[h]
                j = kb - tile * 4
                pi = rot("pp", 4)
                act(Pp[:, pi, c0:T], pb[Sb][:, c0:T], AF.Exp,
                    (("pb", Sb), "Bt"), (("Pp", pi),), bias=Bt[:, kb, h:h + 1], scale=0.125)
                rel(Sb)
                if j >= 0:
                    dve(lambda e: e.tensor_tensor(Pp[:, pi, c0:c0 + 128], Pp[:, pi, c0:c0 + 128], mask01, ALU.mult),
                        (("Pp", pi),) + CONST, (("Pp", pi),))
                mm_group([(pb[Ob][0:65, c0:T], Vc[:, kb, h, :], Pp[:, pi, c0:T], kb == 0, kb == kbmax)],
                         (("Vc", kb), ("Pp", pi)), (("pb", Ob),))

            pend = [qk(*steps[0])]
            if len(steps) > 1:
                pend.append(qk(*steps[1]))
            for i, (h, kb) in enumerate(steps):
                if i + 2 < len(steps):
                    pend.append(qk(*steps[i + 2]))
                Sb, c0 = pend.pop(0)
                softmax_pv(h, kb, Sb, c0)
                main_left -= 1
                nside = -(-len(side) // max(main_left, 1)) if main_left > 0 else len(side)
                for _ in range(min(nside, len(side), 3)):
                    side.popleft()()
                if kb == kbmax:
                    Ob = hctx[h]
                    dve(lambda e: e.tensor_copy(nrm[0:65, 3, :], pb[Ob][0:65, :]), (("pb", Ob),), ("nrmo",))
                    rel(Ob)
                    ts = tail_stages(h)
                    nop = lambda: None
                    if kbmax + 1 >= 12 and h < 7:
                        seq_ = [ts[0], nop, nop, nop, nop, ts[1], nop, nop, ts[2], nop, nop, ts[3]] + ([nop, ts[4]] if len(ts) > 4 else [])
                    else:
                        seq_ = ts
                    for stg in reversed(seq_):
                        side.appendleft(stg)
            while side:
                side.popleft()()

            wof_v = wo_d[0:512, :].rearrange("(h p) n -> p h n", p=128)
            wom_v = wo_d[512:1024, :].rearrange("(h p) n -> p h n", p=128)
            for half in range(2):
                hc = slice(half * 512, (half + 1) * 512)
                sl = []
                for g, wv_ in ((0, wof_v), (2, wom_v)):
                    s = rot("ring", 4)
                    slab = ring[:, s, :].rearrange("p (a b) -> p a b", a=4)
                    sid = 2 * NJ + 14 + half * 3 + g
                    wload(("ring", s), f"s_ring{s}", f"s_rst{s}", ring[:, s, :], wscr[sid, :, :], ("wscr", sid),
                          [(slab[:, :, :], wv_[:, :, hc])])
                    sl.append((s, slab))
                for sub in range(4):
                    cs = slice(sub * 128, (sub + 1) * 128)
                    B = bank()
                    mms = [(pb[B][:, :], yTp(p_)[:, cs], sl[0][1][:, p_, :], p_ == 0, False) for p_ in range(4)]
                    mms += [(pb[B][:, :], yTm(h)[:, cs], sl[1][1][:, h, :], False, h == 3) for h in range(4)]
                    mm_group(mms, tuple(("ring", q[0]) for q in sl) + tuple(("scr", k) for k in (0, 1, 2, 3, 12, 13, 14, 15)),
                             (("pb", B),))
                    dve(lambda e, sub=sub, B=B, hc=hc: e.tensor_tensor(xt[:, sub, hc], xt[:, sub, hc], pb[B][:, :], ALU.add),
                        (("pb", B), ("xt", sub)), (("xt", sub),))
                    rel(B)

            if dbg_stop:
                continue
            norm_T(16)
            ffn(1)

            first["v"] = False
            for sub in range(NSUB):
                act(hb[:, sub, :], xt[:, sub, :], AF.Square, (("xt", sub),), (("hb", sub), ("ssq", sub)), accum=ssq[:, sub:sub + 1])
                act(rstd[:, 4 + sub:5 + sub], ssq[:, sub:sub + 1], AF.Ln, (("ssq", sub),), (("rstd_t", sub),), bias=EPS, scale=1.0 / D)
                act(rstd[:, sub:sub + 1], rstd[:, 4 + sub:5 + sub], AF.Exp, (("rstd_t", sub),), (("rstd", sub),), scale=-0.5)
                oi = rot("ost", 2)
                dve(lambda e: e.scalar_tensor_tensor(ost[:, oi, :], xt[:, sub, :], rstd[:, sub:sub + 1],
                                                     gfin[:], ALU.mult, ALU.mult),
                    (("xt", sub), ("rstd", sub)) + CONST, (("ost", oi),))
                tok = spdma(f"s_out{oi}", [(out_d[seq, t0 + sub * 128:t0 + (sub + 1) * 128, :], ost[:, oi, :])],
                            reads=(("ost", oi),))
                P.final_tokens.append(tok)

    last = {}
    for k, v in P.final_tokens:
        last[k] = max(last.get(k, 0), v)
    P.ops["sp"].append((lambda e: e.nop(), dict(last), "s_fin", False))
    P.count["s_fin"] = 1

    semnames = set(P.count.keys())
    sems = {k: es.enter_context(nc.semaphore(str(k))) for k in sorted(semnames)}
    with es:
        with nc.Block() as block:
            @block.tensor
            def _(e):
                P.emit("pe", e, sems)

            @block.scalar
            def _(e):
                P.emit("act", e, sems)

            @block.vector
            def _(e):
                P.emit("dve", e, sems)

            @block.gpsimd
            def _(e):
                P.emit("pool", e, sems)

            @block.sync
            def _(e):
                P.emit("sp", e, sems)
    return nc


def make_consts():
    cf = np.zeros((128, 5, 128), np.float32)
    cf[:, 0, :] = np.eye(128, dtype=np.float32)
    j = np.arange(128)
    tri = (j[:, None] <= j[None, :]).astype(np.float32)
    cf[:, 1, :] = tri
    cf[:, 2, :] = 1.0
    cf[:, 3, :] = tri * np.float32(128.0 ** -0.5)
    cf[0:64, 4, 0:64] = 1.0 / 64.0
    cf[64, 4, 64:128] = 1.0
    cb = np.zeros((128, 4, 128), ml_dtypes.bfloat16)
    cb[:, 0, :] = np.eye(128).astype(ml_dtypes.bfloat16)
    cb[:, 1, :] = tri.astype(ml_dtypes.bfloat16)
    cb[0, 2, :] = 1.0
    for k_ in range(64):
        cb[k_, 3, 64 + k_] = 1.0
    return cf, cb


def prep_shared(inp):
    f = lambda a: np.ascontiguousarray(np.asarray(a, dtype=np.float32))
    w_in = f(inp["w_in"])[0]
    wfm = np.concatenate([w_in[:, 0:512], w_in[:, 512:1024], w_in[:, 1544:2056], w_in[:, 2056:2568]], axis=1)
    wtm = np.concatenate([w_in[:, 1024:1536], w_in[:, 2568:3080], w_in[:, 3080:3592],
                          w_in[:, 1536:1544], w_in[:, 3592:3596], w_in[:, 3596:3600]], axis=1)
    gcols = np.concatenate([f(inp[k])[0].reshape(8, 128).T for k in ("ffn1_norm", "mix_norm", "ffn2_norm")], axis=1)
    conv = f(inp["conv_w"])[0]
    convc = conv.reshape(4, 8, 128).transpose(2, 1, 0).reshape(128, 32)
    gfox = f(inp["fox_out_norm"])[0].reshape(8, 64).T
    gfin = np.tile(f(inp["final_norm"])[None, :], (128, 1))
    gml = np.tile(f(inp["mlstm_out_norm"])[0][None, :], (128, 1))
    bias16 = np.concatenate([f(inp["fox_f_bias"])[0], f(inp["mlstm_i_bias"])[0], f(inp["mlstm_f_bias"])[0]])
    biasb = np.tile(bias16[None, :], (128, 1))
    cf, cb = make_consts()
    c = np.ascontiguousarray
    return {
        "wgu1": f(inp["ffn1_w_gu"])[0], "wgu2": f(inp["ffn2_w_gu"])[0],
        "wd1": f(inp["ffn1_w_down"])[0], "wd2": f(inp["ffn2_w_down"])[0],
        "wfm": c(wfm), "wtm": c(wtm), "wo": f(inp["w_out"])[0],
        "gcols": c(gcols), "convc": c(convc), "gfox": c(gfox), "gfin": c(gfin), "gml": c(gml),
        "biasb": c(biasb), "cf32": cf, "cbf": cb,
    }


def kernel(**inputs):
    x = np.asarray(inputs["x"], dtype=np.float32)
    Bn, S, _ = x.shape
    n = 8
    nseq = Bn // n
    shared = prep_shared(inputs)
    nc = build(S, nseq)
    in_maps = []
    for cix in range(n):
        m = dict(shared)
        m["x"] = np.ascontiguousarray(x[cix * nseq:(cix + 1) * nseq])
        in_maps.append(m)
    res = run_bass_kernel_spmd(nc, in_maps, core_ids=list(range(n)))
    return np.concatenate([np.asarray(r["out"], dtype=np.float32) for r in res.results], axis=0)
```

```python
import contextlib
import numpy as np
import ml_dtypes
import concourse.bass as bass
import concourse.mybir as mybir
from concourse.bass_utils import run_bass_kernel_spmd

F32 = mybir.dt.float32
BF16 = mybir.dt.bfloat16
AF = mybir.ActivationFunctionType
ALU = mybir.AluOpType
AX = mybir.AxisListType

D = 1024
DFF = 2816
NJ = DFF // 128
EPS = 1e-6
T = 512
NSUB = 4


class Prog:
    def __init__(self):
        self.ops = {e: [] for e in ("pe", "act", "dve", "pool", "sp")}
        self.count = {}
        self.last_w = {}
        self.readers = {}
        self.final_tokens = []

    def op(self, eng, fn, reads=(), writes=(), sem=None, inc=1, per_ins=False):
        waits = {}

        def add(tok):
            if tok is None:
                return
            k, v = tok
            if eng == "pe" and k == "pe":
                return
            if waits.get(k, 0) < v:
                waits[k] = v

        for k in reads:
            add(self.last_w.get(k))
        for k in writes:
            add(self.last_w.get(k))
            for sk, sv in self.readers.get(k, {}).items():
                add((sk, sv))
        semkey = sem if sem is not None else eng
        self.count[semkey] = self.count.get(semkey, 0) + inc
        tok = (semkey, self.count[semkey])
        self.ops[eng].append((fn, waits, semkey, per_ins))
        for k in writes:
            self.last_w[k] = tok
            self.readers[k] = {}
        for k in reads:
            r = self.readers.setdefault(k, {})
            if r.get(tok[0], 0) < tok[1]:
                r[tok[0]] = tok[1]
        return tok

    def emit(self, eng, engine, sems):
        waited = {}
        for fn, waits, semkey, per_ins in self.ops[eng]:
            for k, v in waits.items():
                if waited.get(k, 0) < v:
                    engine.wait_ge(sems[k], v)
                    waited[k] = v
            r = fn(engine)
            if per_ins:
                for ins in r:
                    ins.then_inc(sems[semkey], 16)
            else:
                r.then_inc(sems[semkey], 1)


def build(S, NSEQ, dbg_stop=False):
    NT = S // T
    NKB = S // 128
    nc = bass.Bass("TRN2", target_bir_lowering=False)
    P = Prog()
    es = contextlib.ExitStack()

    def dram(name, shape, dt, kind):
        return nc.dram_tensor(name, shape, dt, kind=kind).ap()

    x_d = dram("x", [NSEQ, S, D], F32, "ExternalInput")
    out_d = dram("out", [NSEQ, S, D], F32, "ExternalOutput")
    wgu_d = [dram("wgu1", [D, 2 * DFF], F32, "ExternalInput"), dram("wgu2", [D, 2 * DFF], F32, "ExternalInput")]
    wd_d = [dram("wd1", [DFF, D], F32, "ExternalInput"), dram("wd2", [DFF, D], F32, "ExternalInput")]
    wfm_d = dram("wfm", [D, 2048], F32, "ExternalInput")
    wtm_d = dram("wtm", [D, 1552], F32, "ExternalInput")
    wo_d = dram("wo", [D, D], F32, "ExternalInput")
    gcols_d = dram("gcols", [128, 24], F32, "ExternalInput")
    convc_d = dram("convc", [128, 32], F32, "ExternalInput")
    gfox_d = dram("gfox", [64, 8], F32, "ExternalInput")
    gfin_d = dram("gfin", [128, D], F32, "ExternalInput")
    gml_d = dram("gml", [128, 512], F32, "ExternalInput")
    biasb_d = dram("biasb", [128, 16], F32, "ExternalInput")
    cf32_d = dram("cf32", [128, 5, 128], F32, "ExternalInput")
    cbf_d = dram("cbf", [128, 4, 128], BF16, "ExternalInput")
    NSLAB = 2 * NJ + 8 + 6 + 6
    wscr = dram("wscr", [NSLAB, 128, 2048], BF16, "Internal")
    wscrd = dram("wscrd", [4 * (NJ // 2), 128, 1024], BF16, "Internal")

    def sb(name, shape, dt):
        return es.enter_context(nc.sbuf_tensor(name, shape, dt))

    xt = sb("xt", [128, NSUB, D], F32)
    ost = sb("ost", [128, 2, D], F32)
    hb = sb("hb", [128, NSUB, D], BF16)
    hT = sb("hT", [128, 8, T], BF16)
    scr = sb("scr", [128, 24, T], BF16)
    sg = sb("sg", [128, 2, T], F32)
    KT = sb("KT", [128, 4, S], BF16)
    Vc = sb("Vc", [128, NKB, 8, 65], BF16)
    pre = sb("pre", [128, 2, T + 3], F32)
    hist = sb("hist", [128, 8, 3], F32)
    cvt = sb("cvt", [128, 1, T], F32)
    mV = sb("mV", [128, NSUB, 4, 129], BF16)
    osig = sb("osig", [128, NSUB, 512], F32)
    gat = sb("gat", [128, NSUB, 16], F32)
    lgf = sb("lgf", [128, NSUB, 16], F32)
    etmp = sb("etmp", [128, NSUB, 16], F32)
    Fcol = sb("Fcol", [128, NKB, 8], F32)
    Cend = sb("Cend", [128, NKB + 1, 8], F32)
    Bt = sb("Bt", [128, NKB, 8], F32)
    D3 = sb("D3", [128, 2, T], BF16)
    dl32 = sb("dl32", [128, 4, 8], F32)
    dlb = sb("dlb", [128, 4, 8], BF16)
    Pp = sb("Pp", [128, 4, T], BF16)
    nrm = sb("nrm", [128, 4, T], F32)
    QTz = hb[:].rearrange("p s (a t) -> p (s a) t", a=2)
    C32 = sb("C32", [128, 4, 129], F32)
    Csb = sb("Csb", [128, 4, 129], BF16)
    WT = sb("WT", [128, 4, 128], BF16)
    Ku = sb("Ku", [128, 4, 128], BF16)
    ytok = sb("ytok", [128, 512], BF16)
    ring = sb("ring", [128, 4, 2048], BF16)
    ringd = sb("ringd", [128, 3, 2, 512], BF16)
    wgt = sb("wgt", [128, 8, 16], BF16)
    gcols = sb("gcols_s", [128, 24], F32)
    convc = sb("convc_s", [128, 32], F32)
    gfox = sb("gfox_s", [64, 8], F32)
    gfin = sb("gfin_s", [128, D], F32)
    gml = sb("gml_s", [128, 512], F32)
    biasb = sb("biasb_s", [128, 16], F32)
    cf32 = sb("cf32_s", [128, 5, 128], F32)
    cbf = sb("cbf_s", [128, 4, 128], BF16)
    ssq = sb("ssq", [128, 8], F32)
    rstd = sb("rstd", [128, 8], F32)
    aS = sb("aS", [128, NSUB, 4], F32)
    uS = sb("uS", [128, NSUB, 4], F32)
    eS = sb("eS", [128, NSUB, 4], F32)
    nb = sb("nb", [128, NSUB, 4], F32)
    zbc = sb("zbc", [128, 32], F32)
    rho = sb("rho", [128, 16], F32)
    rhos = sb("rhos", [128, 16], F32)
    sm = sb("sm", [4, 64], F32)
    dg = sb("dg", [4, 32], F32)
    den = sb("den", [128, 4, 8], F32)

    ident_f = cf32[:, 0, :]
    U_f = cf32[:, 1, :]
    ones_f = cf32[:, 2, :]
    maskS = cf32[:, 3, :]
    statW = cf32[:, 4, :]
    ident_b = cbf[:, 0, :]
    mask01 = cbf[:, 1, :]
    sel0 = cbf[:, 2, :]
    shiftO = cbf[:, 3, :]
    selL = cf32[:, 4, 64:128]

    NB = 6
    pb = [es.enter_context(nc.psum_tensor(f"pb{i}", [128, 512], F32)) for i in range(NB)]
    ptb = [es.enter_context(nc.psum_tensor(f"pt{i}", [128, 1024], BF16)) for i in range(2)]

    actT = scr
    QT = lambda c: scr[:, c, :]
    mqT = lambda h: scr[:, 4 + h, :]
    mkT = lambda h: scr[:, 8 + h, :]
    yTm = lambda h: scr[:, 12 + h, :]
    yTf = lambda h: scr[0:64, 16 + h, :]
    yTp = lambda p_: scr[:, p_, :]

    st = {"bank": 0, "open": set(), "ring": 0, "ringd": 0, "sg": 0, "pp": 0, "pth": 0, "ost": 0}

    def bank():
        for _ in range(NB):
            b = st["bank"]
            st["bank"] = (b + 1) % NB
            if b not in st["open"]:
                st["open"].add(b)
                return b
        raise AssertionError("out of PSUM banks")

    def rel(*bs):
        for b in bs:
            st["open"].discard(b)

    def rot(name, n):
        v = st[name]
        st[name] = (v + 1) % n
        return v

    def mm_group(mms, reads, writes):
        def fn(e, mms=mms):
            r = None
            for (o, l, rh, s0, s1) in mms:
                r = e.matmul(o, l, rh, start=s0, stop=s1)
            return r
        return P.op("pe", fn, reads, writes)

    def tr_group(trs, reads, writes):
        def fn(e, trs=trs):
            r = None
            for (o, i, idn) in trs:
                r = e.transpose(o, i, idn)
            return r
        return P.op("pe", fn, reads, writes)

    def act(out, in_, func, reads, writes, bias=None, scale=None, accum=None):
        kw = {}
        if bias is not None:
            kw["bias"] = bias
        if scale is not None:
            kw["scale"] = scale
        if accum is not None:
            kw["accum_out"] = accum
        return P.op("act", lambda e: e.activation(out, in_, func, **kw), reads, writes)

    class _Rec:
        def __getattr__(self, name):
            def f(*a, **k):
                self.call = (name, a, k)
                return self
            return f

    def dve(f, reads, writes):
        rec = _Rec()
        f(rec)
        name, a, k = rec.call
        return P.op("dve", lambda e: getattr(e, name)(*a, **k), reads, writes)

    def wdma(slot_key, sem, dmas, reads=(), writes=()):
        def fn(e, dmas=dmas):
            return [e.dma_start(out=o, in_=i) for (o, i) in dmas]
        return P.op("pool", fn, reads, tuple(writes) + (slot_key,), sem=sem, inc=16 * len(dmas), per_ins=True)

    first = {"v": True}

    def wload(slot_key, sem, stsem, sb_ap, scr_ap, scr_key, dmas):
        if first["v"]:
            wdma(slot_key, sem, dmas)
            spdma(stsem, [(scr_ap, sb_ap)], reads=(slot_key,), writes=(scr_key,))
        else:
            spdma(sem + "h", [(sb_ap, scr_ap)], reads=(scr_key,), writes=(slot_key,))

    def spdma(sem, dmas, reads=(), writes=()):
        def fn(e, dmas=dmas):
            return [e.dma_start(out=o, in_=i) for (o, i) in dmas]
        return P.op("sp", fn, reads, writes, sem=sem, inc=16 * len(dmas), per_ins=True)

    spdma("s_const", [(gcols[:], gcols_d[:, :]), (convc[:], convc_d[:, :]), (gfox[:], gfox_d[:, :]),
                      (gfin[:], gfin_d[:, :]), (gml[:], gml_d[:, :]), (biasb[:], biasb_d[:, :]),
                      (cf32[:], cf32_d[:, :, :]), (cbf[:], cbf_d[:, :, :])],
          writes=("const",))
    wtm_v = wtm_d.rearrange("(kc p) n -> p kc n", p=128)
    wfm_v = wfm_d.rearrange("(kc p) n -> p kc n", p=128)
    wdma("wgt", "s_wgt", [(wgt[:], wtm_v[:, :, 1536:1552])])
    dve(lambda e: e.memset(Vc[:], 1.0), (), tuple(("Vc", kb) for kb in range(NKB)))
    dve(lambda e: e.memset(mV[:], 1.0), (), ("mV",))
    dve(lambda e: e.memset(D3[:], 0.0), (), (("D3", 0), ("D3", 1)))
    dve(lambda e: e.memset(nrm[:], 0.0), (), ("nrmo", "nrm0", "nrm1", "nrm2"))

    CONST = ("const",)

    def norm_T(goff):
        for sub in range(NSUB):
            act(hb[:, sub, :], xt[:, sub, :], AF.Square, (("xt", sub),), (("hb", sub), ("ssq", sub)), accum=ssq[:, sub:sub + 1])
        for sub in range(NSUB):
            act(rstd[:, 4 + sub:5 + sub], ssq[:, sub:sub + 1], AF.Ln, (("ssq", sub),), (("rstd_t", sub),), bias=EPS, scale=1.0 / D)
            act(rstd[:, sub:sub + 1], rstd[:, 4 + sub:5 + sub], AF.Exp, (("rstd_t", sub),), (("rstd", sub),), scale=-0.5)
            if sub % 2 == 1:
                act(hb[:, sub, :], xt[:, sub, :], AF.Copy, (("xt", sub), ("rstd", sub)), (("hb", sub),), scale=rstd[:, sub:sub + 1])
            else:
                dve(lambda e: e.tensor_scalar(hb[:, sub, :], xt[:, sub, :], rstd[:, sub:sub + 1], None, ALU.mult),
                    (("xt", sub), ("rstd", sub)), (("hb", sub),))
            h = rot("pth", 2)
            tr_group([(ptb[h][:, kc * 128:(kc + 1) * 128], hb[:, sub, kc * 128:(kc + 1) * 128], ident_b) for kc in range(8)],
                     (("hb", sub),) + CONST, (("pt", h),))
            dve(lambda e: e.tensor_tensor(hT[:, :, sub * 128:(sub + 1) * 128],
                                          ptb[h][:, :].rearrange("p (k t) -> p k t", k=8),
                                          gcols[:, goff:goff + 8].unsqueeze(2).to_broadcast([128, 8, 128]), ALU.mult),
                (("pt", h),) + CONST, tuple(("hT", kc) for kc in range(8)))

    def ffn(idx):
        wv = wgu_d[idx].rearrange("(kc p) n -> p kc n", p=128)
        hT_keys = tuple(("hT", kc) for kc in range(8))
        for j in range(NJ):
            s = rot("ring", 4)
            slab = ring[:, s, :].rearrange("p (a b) -> p a b", a=8)
            sid = idx * NJ + j
            wload(("ring", s), f"s_ring{s}", f"s_rst{s}", ring[:, s, :], wscr[sid, :, :], ("wscr", sid),
                  [(slab[:, :, 0:128], wv[:, :, j * 128:(j + 1) * 128]),
                   (slab[:, :, 128:256], wv[:, :, DFF + j * 128:DFF + (j + 1) * 128])])
            G = bank()
            mm_group([(pb[G][:, :], slab[:, kc, 0:128], hT[:, kc, :], kc == 0, kc == 7) for kc in range(8)],
                     (("ring", s),) + hT_keys, (("pb", G),))
            Ub = bank()
            mm_group([(pb[Ub][:, :], slab[:, kc, 128:256], hT[:, kc, :], kc == 0, kc == 7) for kc in range(8)],
                     (("ring", s),) + hT_keys, (("pb", Ub),))
            i = rot("sg", 2)
            act(sg[:, i, :], pb[G][:, :], AF.Silu, (("pb", G),), (("sg", i),))
            dve(lambda e, i=i, Ub=Ub, j=j: e.tensor_tensor(actT[:, j, :], sg[:, i, :], pb[Ub][:, :], ALU.mult),
                (("sg", i), ("pb", Ub)), (("scr", j),))
            rel(G, Ub)
        for half in range(2):
            banks = [bank() for _ in range(4)]
            assert len(set(banks)) == 4
            for jj in range(NJ // 2):
                s = rot("ringd", 3)
                sid = (idx * 2 + half) * (NJ // 2) + jj
                wload(("ringd", s), f"s_ringd{s}", f"s_rdst{s}", ringd[:, s, :, :].rearrange("p a b -> p (a b)"),
                      wscrd[sid, :, :], ("wscrd", sid),
                      [(ringd[:, s, :, :], wd_d[idx][jj * 256:(jj + 1) * 256, half * 512:(half + 1) * 512]
                        .rearrange("(a p) n -> p a n", p=128))])
                for a in range(2):
                    j = jj * 2 + a
                    mm_group([(pb[banks[sub]][:, :], actT[:, j, sub * 128:(sub + 1) * 128], ringd[:, s, a, :], j == 0, j == NJ - 1)
                              for sub in range(4)],
                             (("ringd", s), ("scr", j)), tuple(("pb", b) for b in banks))
            for sub in range(4):
                b = banks[sub]
                dve(lambda e, sub=sub, b=b, half=half: e.scalar_tensor_tensor(
                    xt[:, sub, half * 512:(half + 1) * 512], pb[b][:, :], 0.5, xt[:, sub, half * 512:(half + 1) * 512],
                    ALU.mult, ALU.add),
                    (("pb", b), ("xt", sub)), (("xt", sub),))
            rel(*banks)

    for seq in range(NSEQ):
        dve(lambda e: e.memset(C32[:], 0.0), (), ("C32",))
        dve(lambda e: e.memset(hist[:], 0.0), (), ("hist",))
        dve(lambda e: e.memset(Cend[:, 0, :], 0.0), (), ("Cend",))
        dve(lambda e: e.memset(sm[:], 0.0), (), ("sm",))
        for tile in range(NT):
            t0 = tile * T
            for sub in range(NSUB):
                spdma(f"s_x{sub}", [(xt[:, sub, :], x_d[seq, t0 + sub * 128:t0 + (sub + 1) * 128, :])],
                      writes=(("xt", sub),))
            norm_T(0)
            ffn(0)
            norm_T(8)
            hT_keys = tuple(("hT", kc) for kc in range(8))
            def fm_stage(cp):
                s = rot("ring", 4)
                slab = ring[:, s, :].rearrange("p (a b) -> p a b", a=8)
                sid = 2 * NJ + cp
                wload(("ring", s), f"s_ring{s}", f"s_rst{s}", ring[:, s, :], wscr[sid, :, :], ("wscr", sid),
                      [(slab[:, :, :], wfm_v[:, :, cp * 256:(cp + 1) * 256])])
                for cc in range(2):
                    ch = cp * 2 + cc
                    B = bank()
                    mm_group([(pb[B][:, :], slab[:, kc, cc * 128:(cc + 1) * 128], hT[:, kc, :], kc == 0, kc == 7)
                              for kc in range(8)], (("ring", s),) + hT_keys, (("pb", B),))
                    if ch < 4:
                        act(QTz[0:64, 2 * ch, :], pb[B][0:64, :], AF.Copy, (("pb", B),), (("hb", ch),))
                        act(QTz[64:128, 2 * ch + 1, :], pb[B][64:128, :], AF.Copy, (("pb", B),), (("hb", ch),))
                        dve(lambda e: e.memset(QTz[64:128, 2 * ch, :], 0.0), (), (("hb", ch),))
                        dve(lambda e: e.memset(QTz[0:64, 2 * ch + 1, :], 0.0), (), (("hb", ch),))
                    elif ch < 8:
                        c = ch - 4
                        dve(lambda e, c=c, B=B: e.tensor_copy(KT[:, c, t0:t0 + T], pb[B][:, :]),
                            (("pb", B),), (("KT", c, tile),))
                    else:
                        ci = ch - 8
                        pi = ci % 2
                        dve(lambda e, pi=pi, ci=ci: e.tensor_copy(pre[:, pi, 0:3], hist[:, ci, :]),
                            ("hist",), (("pre", pi),))
                        act(pre[:, pi, 3:3 + T], pb[B][:, :], AF.Copy, (("pb", B),), (("pre", pi),))
                        dve(lambda e, pi=pi, ci=ci: e.tensor_copy(hist[:, ci, :], pre[:, pi, T:T + 3]),
                            (("pre", pi),), ("hist",))
                        dve(lambda e, pi=pi, ci=ci: e.tensor_scalar(cvt[:, 0, :], pre[:, pi, 0:T],
                                                                   convc[:, ci * 4:ci * 4 + 1], None, ALU.mult),
                            (("pre", pi),) + CONST, (("cvt", 0),))
                        for tap in range(1, 4):
                            dve(lambda e, pi=pi, ci=ci, tap=tap: e.scalar_tensor_tensor(
                                cvt[:, 0, :], pre[:, pi, tap:tap + T], convc[:, ci * 4 + tap:ci * 4 + tap + 1],
                                cvt[:, 0, :], ALU.mult, ALU.add),
                                (("pre", pi), ("cvt", 0)) + CONST, (("cvt", 0),))
                        dst = mqT(ci) if ci < 4 else mkT(ci - 4)
                        act(dst, cvt[:, 0, :], AF.Silu, (("cvt", 0),), (("scr", 4 + ci),))
                    rel(B)
            pre_side = []
            for cp in range(8):
                if cp < 4:
                    fm_stage(cp)
                else:
                    fm_stage(cp)
            def tm_stage(g):
                slots = []
                for kh in range(2):
                    s = rot("ring", 4)
                    slab = ring[:, s, :].rearrange("p (a b) -> p a b", a=4)
                    sid = 2 * NJ + 8 + g * 2 + kh
                    wload(("ring", s), f"s_ring{s}", f"s_rst{s}", ring[:, s, :], wscr[sid, :, :], ("wscr", sid),
                          [(slab[:, :, :], wtm_v[:, kh * 4:(kh + 1) * 4, g * 512:(g + 1) * 512])])
                    slots.append((s, slab))
                for sub in range(4):
                    B = bank()
                    mm_group([(pb[B][:, :], hT[:, kc, sub * 128:(sub + 1) * 128], slots[kc // 4][1][:, kc % 4, :], kc == 0, kc == 7)
                              for kc in range(8)],
                             (("ring", slots[0][0]), ("ring", slots[1][0])) + hT_keys, (("pb", B),))
                    kb = tile * 4 + sub
                    if g == 0:
                        act(Vc[:, kb, :, 0:64], pb[B][:, :].rearrange("p (h d) -> p h d", h=8), AF.Copy,
                            (("pb", B),), (("Vc", kb),))
                    elif g == 1:
                        dve(lambda e, sub=sub, B=B: e.tensor_copy(mV[:, sub, :, 0:128],
                                                                  pb[B][:, :].rearrange("p (h d) -> p h d", h=4)),
                            (("pb", B),), ("mV",))
                    else:
                        act(osig[:, sub, :], pb[B][:, :], AF.Sigmoid, (("pb", B),), (("osig", sub),))
                        dve(lambda e, sub=sub: e.tensor_tensor(osig[:, sub, :], osig[:, sub, :], gml[:], ALU.mult),
                            (("osig", sub),) + CONST, (("osig", sub),))
                    rel(B)
            tm_stage(0)
            tm_stage(1)
            tm_stage(2)
            for sub in range(4):
                B = bank()
                mm_group([(pb[B][:, 0:16], hT[:, kc, sub * 128:(sub + 1) * 128], wgt[:, kc, :], kc == 0, kc == 7)
                          for kc in range(8)], ("wgt",) + hT_keys, (("pb", B),))
                dve(lambda e, sub=sub, B=B: e.tensor_tensor(gat[:, sub, :], pb[B][:, 0:16], biasb[:], ALU.add),
                    (("pb", B),) + CONST, ("gat",))
                rel(B)
            act(etmp[:], gat[:], AF.Exp, ("gat",), ("etmp",), scale=-1.0)
            act(lgf[:], etmp[:], AF.Ln, ("etmp",), ("lgf",), bias=1.0)
            dve(lambda e: e.tensor_scalar(lgf[:], lgf[:], -1.0, None, ALU.mult), ("lgf",), ("lgf",))

            for sub in range(4):
                kb = tile * 4 + sub
                B = bank()
                mm_group([(pb[B][:, 0:8], U_f, lgf[:, sub, 0:8], True, True),
                          (pb[B][:, 8:16], ones_f, lgf[:, sub, 0:8], True, True)],
                         ("lgf",) + CONST, (("pb", B),))
                dve(lambda e, kb=kb, B=B: e.tensor_tensor(Fcol[:, kb, :], pb[B][:, 0:8], Cend[:, kb, :], ALU.add),
                    (("pb", B), "Cend"), ("Fcol",))
                dve(lambda e, kb=kb, B=B: e.tensor_tensor(Cend[:, kb + 1, :], pb[B][:, 8:16], Cend[:, kb, :], ALU.add),
                    (("pb", B), "Cend"), ("Cend",))
                rel(B)
            qe = tile * 4 + 4
            dve(lambda e: e.scalar_tensor_tensor(Bt[:, 0:qe, :], Fcol[:, 0:qe, :], -1.0,
                                                 Cend[:, qe:qe + 1, :].to_broadcast([128, qe, 8]), ALU.mult, ALU.add),
                ("Fcol", "Cend"), ("Bt",))
            dve(lambda e: e.tensor_tensor(dl32[:], Cend[:, qe - 3:qe + 1, :],
                                          Cend[:, qe:qe + 1, :].to_broadcast([128, 4, 8]), ALU.subtract),
                ("Cend",), ("dl32",))
            dve(lambda e: e.tensor_scalar(dlb[:], dl32[:], 8.0, None, ALU.mult), ("dl32",), ("dlb",))

            B = bank()
            mm_group([(pb[B][:, sub * 4:(sub + 1) * 4], U_f, lgf[:, sub, 12:16], True, True) for sub in range(4)],
                     ("lgf",) + CONST, (("pb", B),))
            bloc = B
            dve(lambda e, B=B: e.tensor_tensor(aS[:], gat[:, :, 8:12],
                                               pb[B][:, 0:16].rearrange("p (s h) -> p s h", s=4), ALU.subtract),
                (("pb", B), "gat"), ("aS",))
            B2 = bank()
            tr_group([(pb[B2][0:4, sub * 128:(sub + 1) * 128], aS[:, sub, :], ident_f) for sub in range(4)],
                     ("aS",) + CONST, (("pb", B2),))
            B3 = bank()
            tr_group([(pb[B3][0:4, sub * 128:(sub + 1) * 128], lgf[:, sub, 12:16], ident_f) for sub in range(4)],
                     ("lgf",) + CONST, (("pb", B3),))
            dve(lambda e, B2=B2: e.tensor_reduce(sm[:, 0:4], pb[B2][0:4, :].rearrange("p (s t) -> p s t", s=4), AX.X, ALU.max),
                (("pb", B2),), ("sm",))
            rel(B2)
            dve(lambda e, B3=B3: e.tensor_reduce(sm[:, 4:8], pb[B3][0:4, :].rearrange("p (s t) -> p s t", s=4), AX.X, ALU.add),
                (("pb", B3),), ("sm",))
            rel(B3)
            dve(lambda e: e.tensor_copy(sm[:, 8:9], sm[:, 24:25]), ("sm",), ("sm",))
            dve(lambda e: e.tensor_copy(sm[:, 9:12], sm[:, 4:7]), ("sm",), ("sm",))
            if tile == 0:
                pass
            dve(lambda e: e.tensor_tensor_scan(sm[:, 12:16], sm[:, 8:12], sm[:, 0:4], sm[:, 25:26], ALU.add, ALU.max),
                ("sm",), ("sm",))
            dve(lambda e: e.tensor_tensor(sm[:, 16:17], sm[:, 8:9], sm[:, 25:26], ALU.add), ("sm",), ("sm",))
            dve(lambda e: e.tensor_tensor(sm[:, 17:20], sm[:, 9:12], sm[:, 12:15], ALU.add), ("sm",), ("sm",))
            dve(lambda e: e.tensor_tensor(sm[:, 20:24], sm[:, 16:20], sm[:, 12:16], ALU.subtract), ("sm",), ("sm",))
            dve(lambda e: e.tensor_copy(sm[:, 24:25], sm[:, 7:8]), ("sm",), ("sm",))
            dve(lambda e: e.tensor_copy(sm[:, 25:26], sm[:, 15:16]), ("sm",), ("sm",))
            for c in range(4):
                dve(lambda e, c=c: e.tensor_scalar(dg[:, c * 4:(c + 1) * 4], ident_f[0:4, 0:4], sm[:, 12 + c:13 + c], None, ALU.mult),
                    ("sm",) + CONST, ("dg",))
                dve(lambda e, c=c: e.tensor_scalar(dg[:, 16 + c * 4:16 + (c + 1) * 4], ident_f[0:4, 0:4], sm[:, 20 + c:21 + c], None, ALU.mult),
                    ("sm",) + CONST, ("dg",))
            B4 = bank()
            mm_group([(pb[B4][:, 0:32], ones_f[0:4, :], dg[:, :], True, True)], ("dg",) + CONST, (("pb", B4),))
            dve(lambda e, B4=B4: e.tensor_copy(zbc[:], pb[B4][:, 0:32]), (("pb", B4),), ("zbc",))
            rel(B4)
            act(rho[:], zbc[:, 16:32], AF.Exp, ("zbc",), ("rho",))
            dve(lambda e: e.tensor_scalar(rhos[:], rho[:], 128.0 ** -0.5, None, ALU.mult), ("rho",), ("rhos",))
            dve(lambda e: e.tensor_tensor(uS[:], aS[:], zbc[:, 0:16].rearrange("p (s h) -> p s h", s=4), ALU.subtract),
                ("aS", "zbc"), ("uS",))
            act(uS[:], uS[:], AF.Exp, ("uS",), ("uS",))
            dve(lambda e, B=bloc: e.scalar_tensor_tensor(nb[:], pb[B][:, 0:16].rearrange("p (s h) -> p s h", s=4), -1.0,
                                                         zbc[:, 0:16].rearrange("p (s h) -> p s h", s=4),
                                                         ALU.mult, ALU.subtract),
                (("pb", bloc), "zbc"), ("nb",))
            rel(bloc)
            act(eS[:], nb[:], AF.Exp, ("nb",), ("eS",))


            from collections import deque
            side = deque(pre_side)

            def mlstm_stages(sub, h):
                cs = slice(sub * 128, (sub + 1) * 128)
                ch = sub * 4 + h
                bk = {}
                CA, CN, CD = slice(0, 128), slice(128, 257), slice(257, 386)

                def s1():
                    X = bank(); bk["X"] = X
                    dve(lambda e: e.tensor_scalar(Csb[:, h, :], C32[:, h, :], rhos[:, ch:ch + 1], None, ALU.mult),
                        (("C32", h), "rhos"), (("Csb", h),))
                    mm_group([(pb[X][:, CA], mkT(h)[:, cs], mqT(h)[:, cs], True, True)],
                             (("scr", 8 + h), ("scr", 4 + h)), (("pb", X),))
                    hh = rot("pth", 2); bk["t"] = hh
                    tr_group([(ptb[hh][:, 0:128], mkT(h)[:, cs], ident_b)], (("scr", 8 + h),) + CONST, (("pt", hh),))

                def s2():
                    X = bk["X"]; hh = bk["t"]
                    dve(lambda e: e.scalar_tensor_tensor(WT[:, h, :], pb[X][:, CA], uS[:, sub, h:h + 1],
                                                         maskS, ALU.mult, ALU.mult),
                        (("pb", X), "uS") + CONST, (("WT", h),))
                    dve(lambda e: e.tensor_scalar(Ku[:, h, :], ptb[hh][:, 0:128], uS[:, sub, h:h + 1], None, ALU.mult),
                        (("pt", hh), "uS"), (("Ku", h),))

                def s3():
                    X = bk["X"]
                    mm_group([(pb[X][:, CN], WT[:, h, :], mV[:, sub, h, :], True, False),
                              (pb[X][:, CN], mqT(h)[:, cs], Csb[:, h, :], False, True),
                              (pb[X][:, CD], Ku[:, h, :], mV[:, sub, h, :], True, True)],
                             (("WT", h), "mV", ("scr", 4 + h), ("Csb", h), ("Ku", h)), (("pb", X),))

                def s4():
                    X = bk["X"]
                    dve(lambda e: e.scalar_tensor_tensor(C32[:, h, :], C32[:, h, :], rho[:, ch:ch + 1],
                                                         pb[X][:, CD], ALU.mult, ALU.add),
                        (("pb", X), ("C32", h), "rho"), (("C32", h),))
                    nqc = pb[X][:, 256:257]
                    dve(lambda e: e.tensor_scalar(den[:, h, 6:7], nqc, -1.0, eS[:, sub, h:h + 1], ALU.mult, ALU.max),
                        (("pb", X), "eS"), (("den6", h),))
                    dve(lambda e: e.tensor_tensor(den[:, h, 0:1], den[:, h, 6:7], nqc, ALU.max),
                        (("pb", X), ("den6", h)), (("den", h),))
                    dve(lambda e: e.reciprocal(den[:, h, 1:2], den[:, h, 0:1]), (("den", h),), (("den", h),))

                def s5():
                    X = bk["X"]
                    act(ytok[:, h * 128:(h + 1) * 128], pb[X][:, 128:256], AF.Square,
                        (("pb", X), ("den", h)), (("ytok", h), ("den2", h)), scale=den[:, h, 1:2], accum=den[:, h, 2:3])
                    act(den[:, h, 3:4], den[:, h, 2:3], AF.Ln, (("den2", h),), (("den3", h),), bias=EPS, scale=1.0 / 128)
                    act(den[:, h, 4:5], den[:, h, 3:4], AF.Exp, (("den3", h),), (("den4", h),), scale=-0.5)

                def s6():
                    X = bk["X"]
                    dve(lambda e: e.tensor_tensor(den[:, h, 5:6], den[:, h, 4:5], den[:, h, 1:2], ALU.mult),
                        (("den4", h), ("den", h)), (("den5", h),))
                    dve(lambda e: e.scalar_tensor_tensor(ytok[:, h * 128:(h + 1) * 128], pb[X][:, 128:256],
                                                         den[:, h, 5:6], osig[:, sub, h * 128:(h + 1) * 128],
                                                         ALU.mult, ALU.mult),
                        (("pb", X), ("den5", h), ("osig", sub)), (("ytok", h),))
                    rel(X)

                def s7():
                    hh = rot("pth", 2); bk["t"] = hh
                    tr_group([(ptb[hh][:, 0:128], ytok[:, h * 128:(h + 1) * 128], ident_b)],
                             (("ytok", h),) + CONST, (("pt", hh),))

                def s8():
                    hh = bk["t"]
                    dve(lambda e: e.tensor_copy(yTm(h)[:, cs], ptb[hh][:, 0:128]), (("pt", hh),), (("scr", 12 + h),))

                return [s1, s2, s3, s4, s5, s6, s7, s8]

            def tail_stages(h):
                def t1():
                    Lb = bank()
                    mm_group([(pb[Lb][0:64, :], selL, nrm[:, 3, :], True, True)],
                             ("nrmo",) + CONST, (("pb", Lb),))
                    dve(lambda e: e.reciprocal(nrm[0:64, 0, :], pb[Lb][0:64, :]), (("pb", Lb),), ("nrm0",))
                    rel(Lb)

                def t2():
                    dve(lambda e: e.tensor_tensor(nrm[0:64, 1, :], nrm[0:64, 3, :], nrm[0:64, 0, :], ALU.mult),
                        ("nrmo", "nrm0"), ("nrm1",))
                    act(nrm[0:64, 2, :], nrm[0:64, 1, :], AF.Square, ("nrm1",), ("nrm2",))

                def t3():
                    Rb = bank()
                    mm_group([(pb[Rb][0:64, :], statW[:, 0:64], nrm[:, 2, :], True, True)], ("nrm2",) + CONST, (("pb", Rb),))
                    act(nrm[0:64, 2, :], pb[Rb][0:64, :], AF.Ln, (("pb", Rb),), ("nrm2",), bias=EPS)
                    rel(Rb)
                    act(nrm[0:64, 2, :], nrm[0:64, 2, :], AF.Exp, ("nrm2",), ("nrm2",), scale=-0.5)

                def t4():
                    dve(lambda e: e.scalar_tensor_tensor(yTf(h), nrm[0:64, 1, :], gfox[:, h:h + 1],
                                                         nrm[0:64, 2, :], ALU.mult, ALU.mult),
                        ("nrm1", "nrm2") + CONST, (("scr", 16 + h),))
                def t5():
                    Bp = bank()
                    mm_group([(pb[Bp][:, :], ident_b[0:64, :], yTf(h - 1), True, False),
                              (pb[Bp][:, :], shiftO[0:64, :], yTf(h), False, True)],
                             (("scr", 16 + h - 1), ("scr", 16 + h)) + CONST, (("pb", Bp),))
                    dve(lambda e: e.tensor_copy(yTp(h // 2), pb[Bp][:, :]), (("pb", Bp),), (("scr", h // 2),))
                    rel(Bp)
                return [t1, t2, t3, t4] + ([t5] if h % 2 == 1 else [])

            for sub in range(4):
                for hp in range(2):
                    sts = [mlstm_stages(sub, 2 * hp), mlstm_stages(sub, 2 * hp + 1)]
                    for si in range(8):
                        for q_ in range(2):
                            sts[q_][si]()

            kbmax = tile * 4 + 3
            steps = [(h, kb) for h in range(8) for kb in range(kbmax + 1)]
            main_left = len(steps)
            hctx = {}

            def qk(h, kb):
                c = h // 2
                dsl = h % 2
                if kb == 0:
                    dve(lambda e: e.tensor_copy(D3[0:1, dsl, :].rearrange("p (i c) -> p i c", i=4),
                                                dlb[0:1, :, h:h + 1].to_broadcast([1, 4, 128])),
                        ("dlb",), (("D3", dsl),))
                j = kb - tile * 4
                c0 = max(j, 0) * 128
                Sb = bank()
                mm_group([(pb[Sb][:, c0:T], KT[:, c, kb * 128:(kb + 1) * 128], QTz[:, h, c0:T], True, False),
                          (pb[Sb][:, c0:T], sel0, D3[:, dsl, c0:T], False, True)],
                         (("KT", c, kb // 4), ("hb", c), ("D3", dsl)) + CONST, (("pb", Sb),))
                return Sb, c0

            def softmax_pv(h, kb, Sb, c0):
                if kb == 0:
                    hctx[h] = bank()
                Ob = hctx[h]
                j = kb - tile * 4
                pi = rot("pp", 4)
                act(Pp[:, pi, c0:T], pb[Sb][:, c0:T], AF.Exp,
                    (("pb", Sb), "Bt"), (("Pp", pi),), bias=Bt[:, kb, h:h + 1], scale=0.125)
                rel(Sb)
                if j >= 0:
                    dve(lambda e: e.tensor_tensor(Pp[:, pi, c0:c0 + 128], Pp[:, pi, c0:c0 + 128], mask01, ALU.mult),
                        (("Pp", pi),) + CONST, (("Pp", pi),))
                mm_group([(pb[Ob][0:65, c0:T], Vc[:, kb, h, :], Pp[:, pi, c0:T], kb == 0, kb == kbmax)],
                         (("Vc", kb), ("Pp", pi)), (("pb", Ob),))

            pend = [qk(*steps[0])]
            if len(steps) > 1:
                pend.append(qk(*steps[1]))
            for i, (h, kb) in enumerate(steps):
                if i + 2 < len(steps):
                    pend.append(qk(*steps[i + 2]))
                Sb, c0 = pend.pop(0)
                softmax_pv(h, kb, Sb, c0)
                main_left -= 1
                nside = -(-len(side) // max(main_left, 1)) if main_left > 0 else len(side)
                for _ in range(min(nside, len(side), 3)):
                    side.popleft()()
                if kb == kbmax:
                    Ob = hctx[h]
                    dve(lambda e: e.tensor_copy(nrm[0:65, 3, :], pb[Ob][0:65, :]), (("pb", Ob),), ("nrmo",))
                    rel(Ob)
                    ts = tail_stages(h)
                    nop = lambda: None
                    if kbmax + 1 >= 12 and h < 7:
                        seq_ = [ts[0], nop, nop, nop, nop, ts[1], nop, nop, ts[2], nop, nop, ts[3]] + ([nop, ts[4]] if len(ts) > 4 else [])
                    else:
                        seq_ = ts
                    for stg in reversed(seq_):
                        side.appendleft(stg)
            while side:
                side.popleft()()

            wof_v = wo_d[0:512, :].rearrange("(h p) n -> p h n", p=128)
            wom_v = wo_d[512:1024, :].rearrange("(h p) n -> p h n", p=128)
            for half in range(2):
                hc = slice(half * 512, (half + 1) * 512)
                sl = []
                for g, wv_ in ((0, wof_v), (2, wom_v)):
                    s = rot("ring", 4)
                    slab = ring[:, s, :].rearrange("p (a b) -> p a b", a=4)
                    sid = 2 * NJ + 14 + half * 3 + g
                    wload(("ring", s), f"s_ring{s}", f"s_rst{s}", ring[:, s, :], wscr[sid, :, :], ("wscr", sid),
                          [(slab[:, :, :], wv_[:, :, hc])])
                    sl.append((s, slab))
                for sub in range(4):
                    cs = slice(sub * 128, (sub + 1) * 128)
                    B = bank()
                    mms = [(pb[B][:, :], yTp(p_)[:, cs], sl[0][1][:, p_, :], p_ == 0, False) for p_ in range(4)]
                    mms += [(pb[B][:, :], yTm(h)[:, cs], sl[1][1][:, h, :], False, h == 3) for h in range(4)]
                    mm_group(mms, tuple(("ring", q[0]) for q in sl) + tuple(("scr", k) for k in (0, 1, 2, 3, 12, 13, 14, 15)),
                             (("pb", B),))
                    dve(lambda e, sub=sub, B=B, hc=hc: e.tensor_tensor(xt[:, sub, hc], xt[:, sub, hc], pb[B][:, :], ALU.add),
                        (("pb", B), ("xt", sub)), (("xt", sub),))
                    rel(B)

            if dbg_stop:
                continue
            norm_T(16)
            ffn(1)

            first["v"] = False
            for sub in range(NSUB):
                act(hb[:, sub, :], xt[:, sub, :], AF.Square, (("xt", sub),), (("hb", sub), ("ssq", sub)), accum=ssq[:, sub:sub + 1])
                act(rstd[:, 4 + sub:5 + sub], ssq[:, sub:sub + 1], AF.Ln, (("ssq", sub),), (("rstd_t", sub),), bias=EPS, scale=1.0 / D)
                act(rstd[:, sub:sub + 1], rstd[:, 4 + sub:5 + sub], AF.Exp, (("rstd_t", sub),), (("rstd", sub),), scale=-0.5)
                oi = rot("ost", 2)
                dve(lambda e: e.scalar_tensor_tensor(ost[:, oi, :], xt[:, sub, :], rstd[:, sub:sub + 1],
                                                     gfin[:], ALU.mult, ALU.mult),
                    (("xt", sub), ("rstd", sub)) + CONST, (("ost", oi),))
                tok = spdma(f"s_out{oi}", [(out_d[seq, t0 + sub * 128:t0 + (sub + 1) * 128, :], ost[:, oi, :])],
                            reads=(("ost", oi),))
                P.final_tokens.append(tok)

    last = {}
    for k, v in P.final_tokens:
        last[k] = max(last.get(k, 0), v)
    P.ops["sp"].append((lambda e: e.nop(), dict(last), "s_fin", False))
    P.count["s_fin"] = 1

    semnames = set(P.count.keys())
    sems = {k: es.enter_context(nc.semaphore(str(k))) for k in sorted(semnames)}
    with es:
        with nc.Block() as block:
            @block.tensor
            def _(e):
                P.emit("pe", e, sems)

            @block.scalar
            def _(e):
                P.emit("act", e, sems)

            @block.vector
            def _(e):
                P.emit("dve", e, sems)

            @block.gpsimd
            def _(e):
                P.emit("pool", e, sems)

            @block.sync
            def _(e):
                P.emit("sp", e, sems)
    return nc


def make_consts():
    cf = np.zeros((128, 5, 128), np.float32)
    cf[:, 0, :] = np.eye(128, dtype=np.float32)
    j = np.arange(128)
    tri = (j[:, None] <= j[None, :]).astype(np.float32)
    cf[:, 1, :] = tri
    cf[:, 2, :] = 1.0
    cf[:, 3, :] = tri * np.float32(128.0 ** -0.5)
    cf[0:64, 4, 0:64] = 1.0 / 64.0
    cf[64, 4, 64:128] = 1.0
    cb = np.zeros((128, 4, 128), ml_dtypes.bfloat16)
    cb[:, 0, :] = np.eye(128).astype(ml_dtypes.bfloat16)
    cb[:, 1, :] = tri.astype(ml_dtypes.bfloat16)
    cb[0, 2, :] = 1.0
    for k_ in range(64):
        cb[k_, 3, 64 + k_] = 1.0
    return cf, cb


def prep_shared(inp):
    f = lambda a: np.ascontiguousarray(np.asarray(a, dtype=np.float32))
    w_in = f(inp["w_in"])[0]
    wfm = np.concatenate([w_in[:, 0:512], w_in[:, 512:1024], w_in[:, 1544:2056], w_in[:, 2056:2568]], axis=1)
    wtm = np.concatenate([w_in[:, 1024:1536], w_in[:, 2568:3080], w_in[:, 3080:3592],
                          w_in[:, 1536:1544], w_in[:, 3592:3596], w_in[:, 3596:3600]], axis=1)
    gcols = np.concatenate([f(inp[k])[0].reshape(8, 128).T for k in ("ffn1_norm", "mix_norm", "ffn2_norm")], axis=1)
    conv = f(inp["conv_w"])[0]
    convc = conv.reshape(4, 8, 128).transpose(2, 1, 0).reshape(128, 32)
    gfox = f(inp["fox_out_norm"])[0].reshape(8, 64).T
    gfin = np.tile(f(inp["final_norm"])[None, :], (128, 1))
    gml = np.tile(f(inp["mlstm_out_norm"])[0][None, :], (128, 1))
    bias16 = np.concatenate([f(inp["fox_f_bias"])[0], f(inp["mlstm_i_bias"])[0], f(inp["mlstm_f_bias"])[0]])
    biasb = np.tile(bias16[None, :], (128, 1))
    cf, cb = make_consts()
    c = np.ascontiguousarray
    return {
        "wgu1": f(inp["ffn1_w_gu"])[0], "wgu2": f(inp["ffn2_w_gu"])[0],
        "wd1": f(inp["ffn1_w_down"])[0], "wd2": f(inp["ffn2_w_down"])[0],
        "wfm": c(wfm), "wtm": c(wtm), "wo": f(inp["w_out"])[0],
        "gcols": c(gcols), "convc": c(convc), "gfox": c(gfox), "gfin": c(gfin), "gml": c(gml),
        "biasb": c(biasb), "cf32": cf, "cbf": cb,
    }


def kernel(**inputs):
    x = np.asarray(inputs["x"], dtype=np.float32)
    Bn, S, _ = x.shape
    n = 8
    nseq = Bn // n
    shared = prep_shared(inputs)
    nc = build(S, nseq)
    in_maps = []
    for cix in range(n):
        m = dict(shared)
        m["x"] = np.ascontiguousarray(x[cix * nseq:(cix + 1) * nseq])
        in_maps.append(m)
    res = run_bass_kernel_spmd(nc, in_maps, core_ids=list(range(n)))
    return np.concatenate([np.asarray(r["out"], dtype=np.float32) for r in res.results], axis=0)
```
